# Optimizing a Trainium2 kernel written in Bass

```python
import math
import jax, jax.numpy as jnp
from jax import lax
import numpy as np

D_MODEL = 2048
BATCH = 16
SEQ = 2048
DEPTH = 1

CHUNK = 64
D_MIX = D_MODEL
D_REC = D_MIX // 2
REC_BLOCKS = 8
REC_BLOCK_W = D_REC // REC_BLOCKS
CONV_W = 4
LRU_C = 8.0
D_ATT = D_MIX - D_REC
HEAD_DIM = 128
N_HEADS = D_ATT // HEAD_DIM
Q_BLOCK = 128
D_FF = -(-8 * D_MODEL // (3 * 256)) * 256
D_IN = 2 * D_REC + 3 * D_ATT + N_HEADS
EPS = 1e-6

kernel_name = "hymba_rglru_fox_adaln_block"


def rms_norm(x, g):
    xf = x.astype(jnp.float32)
    y = xf * lax.rsqrt(jnp.mean(xf * xf, axis=-1, keepdims=True) + EPS)
    return (y * g.astype(jnp.float32)).astype(x.dtype)


def modulate(h, shift, scale):
    return h * (1.0 + scale[:, None, :]) + shift[:, None, :]


def causal_depthwise_conv(u, w, b):
    S = u.shape[1]
    up = jnp.pad(u, ((0, 0), (CONV_W - 1, 0), (0, 0)))
    y = up[:, 0:S] * w[0]
    for k in range(1, CONV_W):
        y = y + up[:, k:k + S] * w[k]
    return y + b


def block_diag_linear(u, w, b):
    Bn, S, _ = u.shape
    ub = u.reshape(Bn, S, REC_BLOCKS, REC_BLOCK_W)
    y = jnp.einsum('bsnc,ncd->bsnd', ub, w.astype(jnp.float32))
    return y.reshape(Bn, S, D_REC) + b.astype(jnp.float32)


def rglru_group(xr, yr, conv_w, conv_b, w_a, b_a, w_x, b_x, lam):
    u = causal_depthwise_conv(xr, conv_w, conv_b).astype(jnp.float32)
    r = jax.nn.sigmoid(block_diag_linear(u, w_a, b_a))
    i = jax.nn.sigmoid(block_diag_linear(u, w_x, b_x))
    log_a = -LRU_C * r * jax.nn.softplus(-lam.astype(jnp.float32))
    a = jnp.exp(log_a)
    mult = jnp.sqrt(-jnp.expm1(2.0 * log_a))
    b_in = mult * (i * u)

    def combine(left, right):
        a1, h1 = left
        a2, h2 = right
        return a1 * a2, a2 * h1 + h2

    _, h = lax.associative_scan(combine, (a, b_in), axis=1)
    return (h * jax.nn.gelu(yr.astype(jnp.float32))).astype(xr.dtype)


def forgetting_attention_group(q, k, v, f_logit, b_f, g_q, g_k):
    Bn, S, _ = q.shape

    def heads(t):
        return t.reshape(Bn, S, N_HEADS, HEAD_DIM).transpose(0, 2, 1, 3)

    qh = rms_norm(heads(q), g_q).astype(jnp.float32) * (HEAD_DIM ** -0.5)
    kh = rms_norm(heads(k), g_k).astype(jnp.float32)
    vh = heads(v).astype(jnp.float32)
    log_f = jax.nn.log_sigmoid(f_logit.astype(jnp.float32) + b_f.astype(jnp.float32))
    cum = jnp.cumsum(log_f, axis=1).transpose(0, 2, 1)

    outs = []
    for blk in range(S // Q_BLOCK):
        q0 = blk * Q_BLOCK
        q1 = q0 + Q_BLOCK
        qb = qh[:, :, q0:q1]
        kb = kh[:, :, :q1]
        vb = vh[:, :, :q1]
        s = jnp.einsum('bhqd,bhkd->bhqk', qb, kb)
        s = s + cum[:, :, q0:q1, None] - cum[:, :, None, :q1]
        mask = jnp.arange(q0, q1)[:, None] >= jnp.arange(q1)[None, :]
        s = jnp.where(mask, s, -jnp.inf)
        p = jax.nn.softmax(s, axis=-1)
        outs.append(jnp.einsum('bhqk,bhkd->bhqd', p, vb))
    o = jnp.concatenate(outs, axis=2)
    return o.transpose(0, 2, 1, 3).reshape(Bn, S, D_ATT).astype(q.dtype)


def setup_inputs(seed: int = 0) -> dict:
    key = jax.random.key(seed)
    ks = jax.random.split(key, 24)
    f32 = jnp.float32
    nrm = lambda k, shape, s: jax.random.normal(k, shape, f32) * s
    a0 = jax.random.uniform(ks[10], (DEPTH, D_REC), f32, 0.9, 0.999)
    return {
        'x': jax.random.normal(ks[0], (BATCH, SEQ, D_MODEL), f32),
        'c': jax.random.normal(ks[1], (BATCH, D_MODEL), f32),
        'w_ada': nrm(ks[2], (DEPTH, D_MODEL, 6 * D_MODEL), 0.5 * D_MODEL ** -0.5),
        'b_ada': nrm(ks[3], (DEPTH, 6 * D_MODEL), 0.02),
        'g_mix': 1.0 + nrm(ks[4], (DEPTH, D_MODEL), 0.02),
        'w_in': nrm(ks[5], (DEPTH, D_MODEL, D_IN), D_MODEL ** -0.5),
        'conv_w': nrm(ks[6], (DEPTH, CONV_W, D_REC), CONV_W ** -0.5),
        'conv_b': nrm(ks[7], (DEPTH, D_REC), 0.02),
        'w_gate_a': nrm(ks[8], (DEPTH, REC_BLOCKS, REC_BLOCK_W, REC_BLOCK_W), REC_BLOCK_W ** -0.5),
        'b_gate_a': nrm(ks[9], (DEPTH, D_REC), 0.02),
        'w_gate_x': nrm(ks[11], (DEPTH, REC_BLOCKS, REC_BLOCK_W, REC_BLOCK_W), REC_BLOCK_W ** -0.5),
        'b_gate_x': nrm(ks[12], (DEPTH, D_REC), 0.02),
        'lru_logit': jnp.log(a0) - jnp.log1p(-a0),
        'b_forget': 2.0 + nrm(ks[13], (DEPTH, N_HEADS), 0.5),
        'g_q': 1.0 + nrm(ks[14], (DEPTH, HEAD_DIM), 0.02),
        'g_k': 1.0 + nrm(ks[15], (DEPTH, HEAD_DIM), 0.02),
        'g_out_rec': 1.0 + nrm(ks[16], (DEPTH, D_REC), 0.02),
        'g_out_att': 1.0 + nrm(ks[17], (DEPTH, D_ATT), 0.02),
        'w_out': nrm(ks[18], (DEPTH, D_MIX, D_MODEL), D_MIX ** -0.5),
        'g_ffn': 1.0 + nrm(ks[19], (DEPTH, D_MODEL), 0.02),
        'w_up': nrm(ks[20], (DEPTH, D_MODEL, 2 * D_FF), D_MODEL ** -0.5),
        'w_down': nrm(ks[21], (DEPTH, D_FF, D_MODEL), D_FF ** -0.5),
    }


def reference(x, c, w_ada, b_ada, g_mix, w_in, conv_w, conv_b, w_gate_a, b_gate_a,
              w_gate_x, b_gate_x, lru_logit, b_forget, g_q, g_k, g_out_rec, g_out_att,
              w_out, g_ffn, w_up, w_down):
    c_act = jax.nn.silu(c)
    splits = [D_REC, 2 * D_REC, 2 * D_REC + D_ATT, 2 * D_REC + 2 * D_ATT, 2 * D_REC + 3 * D_ATT]
    for l in range(DEPTH):
        mod = c_act @ w_ada[l] + b_ada[l]
        sh1, sc1, gt1, sh2, sc2, gt2 = jnp.split(mod, 6, axis=-1)

        h = modulate(rms_norm(x, g_mix[l]), sh1, sc1)
        proj = h @ w_in[l]
        xr, yr, q, k, v, f_logit = jnp.split(proj, splits, axis=-1)
        y_rec = rglru_group(xr, yr, conv_w[l], conv_b[l], w_gate_a[l], b_gate_a[l],
                            w_gate_x[l], b_gate_x[l], lru_logit[l])
        y_att = forgetting_attention_group(q, k, v, f_logit, b_forget[l], g_q[l], g_k[l])
        mix = jnp.concatenate([rms_norm(y_rec, g_out_rec[l]), rms_norm(y_att, g_out_att[l])], axis=-1)
        x = x + gt1[:, None, :] * (mix @ w_out[l])

        h2 = modulate(rms_norm(x, g_ffn[l]), sh2, sc2)
        gate, up = jnp.split(h2 @ w_up[l], 2, axis=-1)
        x = x + gt2[:, None, :] * ((jax.nn.silu(gate) * up) @ w_down[l])
    return x
```

```python
import numpy as np
import concourse.bass as bass
import concourse.mybir as mybir
from concourse.bass_utils import run_bass_kernel_spmd

F32 = mybir.dt.float32
BF16 = mybir.dt.bfloat16
AF = mybir.ActivationFunctionType
ALU = mybir.AluOpType

ENGS = ("pe", "act", "dve", "pool", "sp")

D = 2048
S = 2048
T = 256
NT = S // T
NSEQ = 2
DFF = 5632
NFF = DFF // 128
EPS = 1e-6
N_ADA = 96
N_CONV = 188
OFF_INS, OFF_INV, OFF_OUT, OFF_UP, OFF_DN = 0, 32, 40, 56, 144
C_PAR, C_BADA, C_CT, C_WG, C_WF, C_CON, C_END = 0, 160, 256, 288, 2336, 2464, 2848


class Prog:
    def __init__(self, nc):
        self.nc = nc
        self.q = {e: [] for e in ENGS}
        self.cnt = {e: 0 for e in ENGS}
        self.sems = {}
        self.seen = {e: {} for e in ENGS}
        self.last_w = {}
        self.readers = {}
        self.slot_cnt = {}
        self.ctx = []

    def sem(self, name):
        if name not in self.sems:
            cm = self.nc.semaphore(name)
            self.ctx.append(cm)
            self.sems[name] = cm.__enter__()
        return self.sems[name]

    def close(self):
        for cm in reversed(self.ctx):
            cm.__exit__(None, None, None)

    def _deps(self, eng, reads, writes, extra):
        deps = []
        for k in reads:
            t = self.last_w.get(k)
            if t is not None:
                deps.append(t)
        for k in writes:
            t = self.last_w.get(k)
            if t is not None and t[2] != eng:
                deps.append(t)
            for r in self.readers.get(k, ()):
                if r[2] != eng:
                    deps.append(r)
        deps.extend(extra)
        return deps

    def _commit(self, tok, reads, writes):
        for k in writes:
            self.last_w[k] = tok
            self.readers[k] = []
        for k in reads:
            self.readers.setdefault(k, []).append(tok)

    def _waits(self, eng, deps):
        best = {}
        for d in deps:
            if d is None:
                continue
            name, val, _ = d
            if self.seen[eng].get(name, 0) >= val:
                continue
            if best.get(name, 0) < val:
                best[name] = val
        for name, val in best.items():
            self.seen[eng][name] = val
        return list(best.items())

    def emit(self, eng, fn, reads=(), writes=(), extra=(), mark=True):
        deps = self._deps(eng, reads, writes, extra)
        waits = self._waits(eng, deps)
        tok = None
        if mark:
            self.cnt[eng] += 1
            tok = ("e_" + eng, self.cnt[eng], eng)
        self.q[eng].append((waits, fn, ("e_" + eng, 1) if mark else None))
        if mark:
            self._commit(tok, reads, writes)
        return tok

    def dma(self, eng, slot, pairs, reads=(), writes=(), extra=()):
        deps = self._deps("dma_none", reads, writes, extra)
        waits = self._waits(eng, deps)
        name = "d_" + slot
        self.sem(name)
        for i, (o, a) in enumerate(pairs):
            def fn(e, o=o, a=a):
                return e.dma_start(out=o, in_=a)
            self.q[eng].append((waits if i == 0 else [], fn, (name, 16)))
        self.slot_cnt[name] = self.slot_cnt.get(name, 0) + 16 * len(pairs)
        tok = (name, self.slot_cnt[name], "dma")
        self._commit(tok, reads, writes)
        return tok

    def wait_all(self, eng, toks):
        waits = self._waits(eng, toks)
        self.q[eng].append((waits, None, None))

    def run(self):
        nc = self.nc
        for e in ENGS:
            self.sem("e_" + e)
        sems = self.sems
        q = self.q

        def play(engine, lst):
            for waits, fn, inc in lst:
                for name, val in waits:
                    engine.wait_ge(sems[name], val)
                if fn is None:
                    continue
                ins = fn(engine)
                if inc is not None:
                    ins.then_inc(sems[inc[0]], inc[1])

        with nc.Block() as block:
            @block.sync
            def _(e):
                play(e, q["sp"])

            @block.tensor
            def _(e):
                play(e, q["pe"])

            @block.scalar
            def _(e):
                play(e, q["act"])

            @block.vector
            def _(e):
                play(e, q["dve"])

            @block.gpsimd
            def _(e):
                play(e, q["pool"])


DBG = None


def build_nc():
    nc = bass.Bass("TRN2", target_bir_lowering=False)
    dbg = DBG
    if dbg:
        d32 = nc.dram_tensor("d32", [128, 32768], F32, kind="ExternalOutput").ap()
        d16 = nc.dram_tensor("d16", [128, 65536], BF16, kind="ExternalOutput").ap()
    dpos = {"32": 0, "16": 0}
    dmap = {}

    def dump(P, name, ap, keys, np_=128):
        if not dbg:
            return
        is16 = ap.dtype == BF16
        kk = "16" if is16 else "32"
        n = 1
        for d_ in ap.shape[1:]:
            n *= d_
        o = dpos[kk]
        dpos[kk] += n
        dmap[name] = (kk, o, tuple(ap.shape))
        dst = (d16 if is16 else d32)[0:np_, o:o + n]
        if len(ap.shape) == 3:
            dst = dst.rearrange("p (a b) -> p a b", a=ap.shape[1])
        P.dma("pool", "dbg", [(dst, ap)], reads=keys, writes=[("dbgout", name)])
    build_nc.dmap = dmap

    x_d = nc.dram_tensor("x", [NSEQ, S, D], F32, kind="ExternalInput").ap()
    wsrc = nc.dram_tensor("wsrc", [N_ADA + N_CONV, 128, 2048], F32, kind="ExternalInput").ap()
    small_d = nc.dram_tensor("small", [128, C_END], F32, kind="ExternalInput").ap()
    out_d = nc.dram_tensor("out", [NSEQ, S, D], F32, kind="ExternalOutput").ap()
    wbf = nc.dram_tensor("wbf", [N_CONV, 128, 2048], BF16, kind="Internal").ap()

    P = Prog(nc)
    E = P.emit
    import contextlib
    with contextlib.ExitStack() as es:
        def sb(name, shape, dt):
            return es.enter_context(nc.sbuf_tensor("sb_" + name, shape, dt))

        def pst(name):
            return es.enter_context(nc.psum_tensor(name, [128, 512], F32))

        small = sb("small", [128, C_END], F32)
        der = sb("der", [128, 64], F32)
        cact = sb("cact", [128, 32], F32)
        ctmp = sb("ctmp", [128, 64], F32)
        modT = sb("modT", [128, 192], F32)
        ABt = sb("ABt", [128, 64], F32)
        xt = sb("xt", [128, 2 * D], F32)
        hT = sb("hT", [128, 16 * T], BF16)
        xn = sb("xn", [128, D], BF16)
        KT = sb("KT", [128, 8 * S], BF16)
        Vt = sb("Vt", [128, 16 * 8 * 130], BF16)
        R1 = sb("R1", [128, 7192], F32)
        tmpA = sb("tmpA", [128, 12 * T], F32)
        ubf = sb("ubf", [128, T], BF16)
        sqb = sb("sqb", [128, 2 * T], BF16)
        rrt = sb("rrt", [128, 2 * T], F32)
        PT = sb("PT", [128, 3 * T], BF16)
        Wr = sb("Wr", [128, 4 * 2048], BF16)
        stg0 = sb("stg0", [128, 2048], F32)
        gtbc = sb("gtbc", [128, 2 * D], F32)
        tmpE = sb("tmpE", [128, 2 * 512], F32)
        Gt = sb("Gt", [128, 2 * 128], F32)
        wgb = sb("wgb", [128, 2048], BF16)
        wfb = sb("wfb", [128, 128], BF16)
        idb = sb("idb", [128, 128], BF16)
        trib = sb("trib", [128, 128], BF16)
        onesb = sb("onesb", [128, 128], BF16)
        cumT = sb("cumT", [128, 16 * 8], F32)
        biasT = sb("biasT", [128, 8 * 16], F32)
        stat = sb("stat", [128, 32], F32)
        cumr = sb("cumr", [8, 2 * T], F32)
        hcar = sb("hcar", [128, 8], F32)
        ccar = sb("ccar", [8, 2], F32)
        bcs = sb("bcs", [128, 16], F32)
        o256 = sb("o256", [8, T], F32)
        halo = sb("halo", [128, 24], F32)
        ps = [pst("ps%d" % i) for i in range(8)]

        par = small[:, C_PAR:C_BADA]
        bada = small[:, C_BADA:C_CT]
        cTt = small[:, C_CT:C_WG]
        wg32 = small[:, C_WG:C_WF]
        wf32 = small[:, C_WF:C_CON]
        ident_f = small[:, C_CON:C_CON + 128]
        tri_f = small[:, C_CON + 128:C_CON + 256]
        ones_f = small[:, C_CON + 256:C_CON + 384]

        stg = [stg0[:], Wr[:, 0:4096].bitcast(F32), Wr[:, 4096:8192].bitcast(F32)]
        cbf = [gtbc[:, i * 1024:(i + 1) * 1024].bitcast(BF16) for i in range(3)]
        Wslot = [Wr[:, i * 2048:(i + 1) * 2048] for i in range(4)]
        xr = R1[:, 0:2072].rearrange("p (j t) -> p j t", j=8)
        gy = R1[:, 2072:4120].rearrange("p (j t) -> p j t", j=8)
        Ot = R1[:, 4120:6168].rearrange("p (s f) -> p s f", s=2)
        qT = R1[:, 6168:7192].bitcast(BF16).rearrange("p (h t) -> p h t", h=8)
        actT = R1[:, 0:5632].bitcast(BF16).rearrange("p (j t) -> p j t", j=NFF)
        hT3 = hT[:].rearrange("p (c t) -> p c t", c=16)
        xt3 = xt[:].rearrange("p (s f) -> p s f", s=2)
        KT3 = KT[:].rearrange("p (h t) -> p h t", h=8)
        Vt4 = Vt[:].rearrange("p (b h d) -> p b h d", b=16, h=8)
        tA = tmpA[:].rearrange("p (k t) -> p k t", k=12)
        halo3 = halo[:].rearrange("p (j k) -> p j k", j=8)
        PT3 = PT[:].rearrange("p (k t) -> p k t", k=3)
        gt3 = gtbc[:].rearrange("p (v f) -> p v f", v=2)
        tE = tmpE[:].rearrange("p (k f) -> p k f", k=2)
        cumT3 = cumT[:].rearrange("p (b h) -> p b h", b=16)
        biasT3 = biasT[:].rearrange("p (h b) -> p h b", h=8)
        modT3 = modT[:].rearrange("p (c b) -> p c b", b=2)
        AB4 = ABt[:].rearrange("p (b w c) -> p b w c", b=2, w=2)
        psb = [p_[:].bitcast(BF16) for p_ in ps]

        P.dma("sp", "small", [(small[:], small_d[:, :])], writes=["small"])
        E("dve", lambda e: e.tensor_copy(out=idb[:], in_=ident_f), reads=["small"], writes=["idb"])
        E("dve", lambda e: e.tensor_copy(out=trib[:], in_=tri_f), reads=["small"], writes=["trib"])
        E("dve", lambda e: e.tensor_copy(out=onesb[:], in_=ones_f), reads=["small"], writes=["onesb"])
        E("dve", lambda e: e.tensor_copy(out=wgb[:], in_=wg32), reads=["small"], writes=["wgb"])
        E("dve", lambda e: e.tensor_copy(out=wfb[:], in_=wf32), reads=["small"], writes=["wfb"])
        E("pool", lambda e: e.memset(Vt[:], 1.0), writes=["Vt"])
        E("pool", lambda e: e.memset(o256[:], 1.0), writes=["ones256"])
        E("act", lambda e: e.activation(out=ctmp[:, 0:8], in_=par[:, 88:96], func=AF.Exp, scale=-1.0),
          reads=["small"], writes=["ctmp"])
        E("act", lambda e: e.activation(out=ctmp[:, 8:16], in_=ctmp[:, 0:8], func=AF.Ln, bias=1.0),
          reads=["ctmp"], writes=["ctmp2"])
        E("dve", lambda e: e.tensor_scalar(out=der[:, 0:8], in0=ctmp[:, 8:16], scalar1=-8.0, scalar2=None, op0=ALU.mult),
          reads=["ctmp2"], writes=["der"])
        E("dve", lambda e: e.tensor_scalar(out=der[:, 8:16], in0=ctmp[:, 8:16], scalar1=-16.0, scalar2=None, op0=ALU.mult),
          reads=["ctmp2"], writes=["der"])
        E("dve", lambda e: e.tensor_scalar(out=der[:, 16:32], in0=par[:, 72:88], scalar1=-1.0, scalar2=None, op0=ALU.mult),
          reads=["small"], writes=["der"])
        E("dve", lambda e: e.tensor_scalar(out=der[:, 32:33], in0=par[:, 112:113], scalar1=128.0 ** -0.5, scalar2=None, op0=ALU.mult),
          reads=["small"], writes=["der"])
        E("dve", lambda e: e.tensor_scalar(out=der[:, 33:34], in0=par[:, 114:115], scalar1=-1.0, scalar2=None, op0=ALU.mult),
          reads=["small"], writes=["der"])
        E("act", lambda e: e.activation(out=ctmp[:, 16:48], in_=cTt, func=AF.Exp, scale=-1.0), reads=["small"], writes=["ctmp3"])
        E("act", lambda e: e.activation(out=ctmp[:, 16:48], in_=ctmp[:, 16:48], func=AF.Ln, bias=1.0), reads=["ctmp3"], writes=["ctmp3"])
        E("act", lambda e: e.activation(out=ctmp[:, 16:48], in_=ctmp[:, 16:48], func=AF.Exp, scale=-1.0), reads=["ctmp3"], writes=["ctmp3"])
        E("dve", lambda e: e.tensor_tensor(out=cact[:], in0=ctmp[:, 16:48], in1=cTt, op=ALU.mult), reads=["ctmp3", "small"], writes=["cact"])

        def SK(sl):
            return [("stg", sl)] + ([("W", 2 * sl - 2), ("W", 2 * sl - 1)] if sl > 0 else [])

        def CK(cs):
            return [("cbf", cs), ("gt", 0), ("gt", 1)]

        for nb in range(N_ADA):
            sl = nb % 3
            P.dma("sp", "stg%d" % sl, [(stg[sl], wsrc[nb, :, :])], writes=SK(sl))
            for c in range(16):
                E("pe", lambda e, sl=sl, c=c, nb=nb: e.matmul(ps[0][:, 2 * nb:2 * nb + 2], lhsT=stg[sl][:, c * 128:(c + 1) * 128],
                                                              rhs=cact[:, 2 * c:2 * c + 2], start=(c == 0), stop=(c == 15)),
                  reads=SK(sl) + ["cact"] if c in (0, 15) else (), writes=[("ps", 0)] if c == 15 else (), mark=(c == 15))
        for b in range(2):
            E("dve", lambda e, b=b: e.tensor_tensor(out=modT3[:, :, b], in0=ps[0][:, 0:192].rearrange("p (c b) -> p c b", b=2)[:, :, b],
                                                    in1=bada, op=ALU.add), reads=[("ps", 0), "small"], writes=["modT"])
            E("dve", lambda e, b=b: e.scalar_tensor_tensor(out=AB4[:, b, 0, :], in0=modT3[:, 16:32, b], scalar=1.0, in1=par[:, 0:16],
                                                           op0=ALU.add, op1=ALU.mult), reads=["modT"], writes=["AB"])
            E("dve", lambda e, b=b: e.scalar_tensor_tensor(out=AB4[:, b, 1, :], in0=modT3[:, 64:80, b], scalar=1.0, in1=par[:, 16:32],
                                                           op0=ALU.add, op1=ALU.mult), reads=["modT"], writes=["AB"])

        for k in range(N_CONV):
            g = N_ADA + k
            sl = g % 3
            cs = k % 3
            P.dma("sp", "stg%d" % sl, [(stg[sl], wsrc[g, :, :])], writes=SK(sl))
            if k % 2 == 0:
                E("act", lambda e, sl=sl, cs=cs: e.activation(out=cbf[cs], in_=stg[sl], func=AF.Copy),
                  reads=SK(sl), writes=CK(cs))
            else:
                E("dve", lambda e, sl=sl, cs=cs: e.tensor_copy(out=cbf[cs], in_=stg[sl]),
                  reads=SK(sl), writes=CK(cs))
            P.dma("pool", "cbf%d" % cs, [(wbf[k, :, :], cbf[cs])], reads=CK(cs), writes=[("wbf", k)])

        wctr = [0]

        def wload(k):
            sl = wctr[0] % 4
            wctr[0] += 1
            P.dma("sp", "W%d" % sl, [(Wslot[sl], wbf[k, :, :])], reads=[("wbf", k)], writes=[("W", sl)])
            return sl

        psrot = [0]

        def nextbank(lo, n):
            b = lo + psrot[0] % n
            psrot[0] += 1
            return b

        def sigmoid_from(out_ap, in_ap, t1, rkeys, wkey, neg_bias=None, scale=1.0):
            kw = {} if neg_bias is None else {"bias": neg_bias}
            E("act", lambda e: e.activation(out=t1, in_=in_ap, func=AF.Exp, scale=-scale, **kw), reads=rkeys, writes=[wkey + "_t", wkey])
            E("act", lambda e: e.activation(out=t1, in_=t1, func=AF.Ln, bias=1.0), reads=[wkey + "_t"], writes=[wkey + "_t"])
            return E("act", lambda e: e.activation(out=out_ap, in_=t1, func=AF.Exp, scale=-1.0), reads=[wkey + "_t"], writes=[wkey])

        def norm_to_hT(b, which, first_extra_keys):
            Bcol0 = 0 if which == 0 else 48
            for s in range(2):
                E("act", lambda e, s=s: e.activation(out=stg0[:], in_=xt3[:, s, :], func=AF.Square, accum_out=stat[:, s:s + 1]),
                  reads=[("xt", s)], writes=[("stg", 0), ("stat", s)])
                E("act", lambda e, s=s: e.activation(out=stat[:, 2 + s:3 + s], in_=stat[:, s:s + 1], func=AF.Ln, scale=1.0 / D, bias=EPS),
                  reads=[("stat", s)], writes=[("stat2", s)])
                E("act", lambda e, s=s: e.activation(out=stat[:, 4 + s:5 + s], in_=stat[:, 2 + s:3 + s], func=AF.Exp, scale=-0.5),
                  reads=[("stat2", s)], writes=[("stat3", s)])
                E("act", lambda e, s=s: e.activation(out=xn[:], in_=xt3[:, s, :], func=AF.Copy, scale=stat[:, 4 + s:5 + s]),
                  reads=[("xt", s), ("stat3", s)], writes=["xn"])
                for g in range(2):
                    bk = 6 + g
                    for c8 in range(8):
                        c = g * 8 + c8
                        E("pe", lambda e, c=c, c8=c8, bk=bk: e.transpose(out=psb[bk][:, c8 * 128:(c8 + 1) * 128],
                                                                         in_=xn[:, c * 128:(c + 1) * 128], identity=idb[:]),
                          reads=["xn", "idb"] if c8 in (0, 7) else (), writes=[("ps", bk)] if c8 in (0, 7) else (), mark=(c8 == 7))
                    for c8 in range(8):
                        c = g * 8 + c8
                        E("dve", lambda e, c=c, c8=c8, bk=bk, s=s: e.tensor_scalar(
                            out=hT3[:, c, s * 128:(s + 1) * 128], in0=psb[bk][:, c8 * 128:(c8 + 1) * 128],
                            scalar1=AB4[:, b, which, c:c + 1], scalar2=modT3[:, Bcol0 + c, b:b + 1], op0=ALU.mult, op1=ALU.add),
                          reads=[("ps", bk), "AB", "modT"], writes=["hT"])

        dump(P, "modT", modT[:], ["modT"])
        dump(P, "AB", ABt[:], ["AB"])
        dump(P, "der", der[:], ["der"])
        for b in range(1 if dbg else NSEQ):
            for v in range(2):
                ch0 = 32 if v == 0 else 80
                for cc in range(16):
                    gsl = cc % 2
                    E("dve", lambda e, gsl=gsl, ch=ch0 + cc, b=b: e.tensor_scalar(out=Gt[:, gsl * 128:(gsl + 1) * 128], in0=ones_f,
                                                                           scalar1=modT3[:, ch, b:b + 1], scalar2=None, op0=ALU.mult),
                      reads=["modT", "small"], writes=[("Gt", gsl)])
                    E("pe", lambda e, gsl=gsl, cc=cc: e.matmul(ps[5][:, (cc % 4) * 128:(cc % 4 + 1) * 128], lhsT=Gt[:, gsl * 128:(gsl + 1) * 128],
                                                               rhs=ident_f, start=True, stop=True),
                      reads=[("Gt", gsl), "small"], writes=[("ps", 5)])
                    if cc % 4 == 3:
                        q4 = cc // 4
                        E("act", lambda e, v=v, q4=q4: e.activation(out=gt3[:, v, q4 * 512:(q4 + 1) * 512], in_=ps[5][:], func=AF.Copy),
                          reads=[("ps", 5)], writes=[("gt", v)])
            dump(P, "gt", gtbc[:], [("gt", 0), ("gt", 1)])
            for i in range(1 if dbg else NT):
                t0 = i * T
                P.dma("pool", "xin", [(xt3[:, s, :], x_d[b, t0 + s * 128:t0 + (s + 1) * 128, :]) for s in range(2)],
                      writes=[("xt", 0), ("xt", 1)])
                norm_to_hT(b, 0, ())
                dump(P, "hT1", hT[:], ["hT"])
                for j in range(32):
                    sl = wload(OFF_INS + j)
                    bk = nextbank(0, 4)
                    for c in range(16):
                        E("pe", lambda e, sl=sl, c=c, bk=bk: e.matmul(ps[bk][:, 0:T], lhsT=Wslot[sl][:, c * 128:(c + 1) * 128],
                                                                      rhs=hT3[:, c, :], start=(c == 0), stop=(c == 15)),
                          reads=[("W", sl), "hT"] if c in (0, 15) else (), writes=[("ps", bk)] if c in (0, 15) else (), mark=(c == 15))
                    pj = ps[bk][:, 0:T]
                    if j < 8:
                        first = ["R1ph"] if j == 0 else []
                        E("act", lambda e, j=j, pj=pj: e.activation(out=xr[:, j, 3:3 + T], in_=pj, func=AF.Copy),
                          reads=[("ps", bk)] + ([] if j == 0 else ["R1ph"]), writes=[("xr", j)] + first)
                        if i == 0:
                            E("pool", lambda e, j=j: e.memset(xr[:, j, 0:3], 0.0), reads=[("xr", j)], writes=[("xrh", j)])
                        else:
                            E("pool", lambda e, j=j: e.tensor_copy(out=xr[:, j, 0:3], in_=halo3[:, j, :]), reads=[("xr", j), ("halo", j)], writes=[("xrh", j)])
                    elif j < 16:
                        jj = j - 8
                        E("act", lambda e, pj=pj: e.activation(out=tA[:, 8, :], in_=pj, func=AF.Square), reads=[("ps", bk), "R1ph"], writes=["g1"])
                        E("dve", lambda e, pj=pj: e.scalar_tensor_tensor(out=tA[:, 8, :], in0=tA[:, 8, :], scalar=1.0 / 0.044715, in1=pj,
                                                                         op0=ALU.add, op1=ALU.mult), reads=["g1", ("ps", bk)], writes=["g2"])
                        sigmoid_from(tA[:, 9, :], tA[:, 8, :], tA[:, 9, :], ["g2"], "g3", scale=1.5957691216 * 0.044715)
                        E("dve", lambda e, jj=jj, pj=pj: e.tensor_tensor(out=gy[:, jj, :], in0=tA[:, 9, :], in1=pj, op=ALU.mult),
                          reads=["g3", ("ps", bk), "R1ph"], writes=[("gy", jj)])
                    else:
                        h = (j - 16) % 8
                        isq = j < 24
                        E("act", lambda e, pj=pj: e.activation(out=sqb[:, 0:T], in_=pj, func=AF.Square), reads=[("ps", bk)], writes=["sqb"])
                        E("pe", lambda e: e.matmul(ps[4][:, 0:T], lhsT=onesb[:], rhs=sqb[:, 0:T], start=True, stop=True),
                          reads=["sqb", "onesb"], writes=[("ps", 4)])
                        E("act", lambda e: e.activation(out=rrt[:, 0:T], in_=ps[4][:, 0:T], func=AF.Ln, scale=1.0 / 128, bias=EPS),
                          reads=[("ps", 4)], writes=["rrt"])
                        E("act", lambda e: e.activation(out=rrt[:, 0:T], in_=rrt[:, 0:T], func=AF.Exp, scale=-0.5), reads=["rrt"], writes=["rrt"])
                        if isq:
                            E("dve", lambda e, h=h, pj=pj: e.scalar_tensor_tensor(out=qT[:, h, :], in0=pj, scalar=der[:, 32:33], in1=rrt[:, 0:T],
                                                                                  op0=ALU.mult, op1=ALU.mult),
                              reads=[("ps", bk), "rrt", "der", "R1ph"], writes=[("qT", h)])
                        else:
                            E("dve", lambda e, h=h, pj=pj, t0=t0: e.scalar_tensor_tensor(out=KT3[:, h, t0:t0 + T], in0=pj, scalar=par[:, 113:114], in1=rrt[:, 0:T],
                                                                                  op0=ALU.mult, op1=ALU.mult),
                              reads=[("ps", bk), "rrt", "small"], writes=[("KT", h)])
                for ng in range(2):
                    for kg in range(4):
                        sl = wload(OFF_INV + ng * 4 + kg)
                        for cc in range(4):
                            c = kg * 4 + cc
                            for s in range(2):
                                first = (kg == 0 and cc == 0)
                                last = (kg == 3 and cc == 3)
                                E("pe", lambda e, sl=sl, cc=cc, c=c, s=s, first=first, last=last: e.matmul(
                                    ps[s][:, :], lhsT=hT3[:, c, s * 128:(s + 1) * 128], rhs=Wslot[sl][:, cc * 512:(cc + 1) * 512],
                                    start=first, stop=last),
                                  reads=[("W", sl), "hT"] if cc in (0, 3) else (), writes=[("ps", s)] if (first or last) else (),
                                  mark=(cc == 3))
                    for s in range(2):
                        blk = t0 // 128 + s
                        E("act", lambda e, s=s, blk=blk, ng=ng: e.activation(
                            out=Vt4[:, blk, ng * 4:(ng + 1) * 4, 0:128], in_=ps[s][:, :].rearrange("p (h d) -> p h d", h=4), func=AF.Copy),
                          reads=[("ps", s)], writes=["Vt"])
                for c in range(16):
                    E("pe", lambda e, c=c: e.matmul(ps[2][0:8, 0:T], lhsT=wfb[:, c * 8:(c + 1) * 8], rhs=hT3[:, c, :], start=(c == 0), stop=(c == 15)),
                      reads=["wfb", "hT"] if c in (0, 15) else (), writes=[("ps", 2)] if c in (0, 15) else (), mark=(c == 15))
                E("act", lambda e: e.activation(out=cumr[:, 0:T], in_=ps[2][0:8, 0:T], func=AF.Exp, scale=-1.0, bias=der[0:8, 33:34]),
                  reads=[("ps", 2), "der"], writes=["cumr0"])
                E("act", lambda e: e.activation(out=cumr[:, 0:T], in_=cumr[:, 0:T], func=AF.Ln, bias=1.0), reads=["cumr0"], writes=["cumr0"])
                if i == 0:
                    E("dve", lambda e: e.tensor_tensor_scan(out=cumr[:, T:2 * T], data0=o256[:], data1=cumr[:, 0:T], initial=0.0,
                                                            op0=ALU.mult, op1=ALU.subtract),
                      reads=["cumr0", "ones256"], writes=["cumr1"])
                else:
                    E("dve", lambda e: e.tensor_tensor_scan(out=cumr[:, T:2 * T], data0=o256[:], data1=cumr[:, 0:T], initial=ccar[:, 0:1],
                                                            op0=ALU.mult, op1=ALU.subtract),
                      reads=["cumr0", "ccar", "ones256"], writes=["cumr1"])
                E("dve", lambda e: e.tensor_copy(out=ccar[:, 0:1], in_=cumr[:, 2 * T - 1:2 * T]), reads=["cumr1"], writes=["ccar"])
                for s in range(2):
                    blk = t0 // 128 + s
                    E("pe", lambda e, s=s: e.transpose(out=ps[3][:, s * 8:(s + 1) * 8], in_=cumr[:, T + s * 128:T + (s + 1) * 128], identity=ident_f[0:8, 0:8]),
                      reads=["cumr1", "small"], writes=[("ps", 3)])
                    E("dve", lambda e, s=s, blk=blk: e.tensor_copy(out=cumT3[:, blk, :], in_=ps[3][:, s * 8:(s + 1) * 8]),
                      reads=[("ps", 3)], writes=["cumT"])
                E("dve", lambda e: e.tensor_scalar(out=ctmp[0:8, 48:56], in0=ident_f[0:8, 0:8], scalar1=cumr[:, T + 127:T + 128], scalar2=None, op0=ALU.mult),
                  reads=["cumr1", "small"], writes=["dg"])
                E("pe", lambda e: e.matmul(ps[3][:, 16:24], lhsT=ones_f[0:8, :], rhs=ctmp[0:8, 48:56], start=True, stop=True),
                  reads=["dg", "small"], writes=[("ps", 3)])
                E("dve", lambda e: e.tensor_copy(out=bcs[:, 0:8], in_=ps[3][:, 16:24]), reads=[("ps", 3)], writes=["bcs"])
                nkb = t0 // 128 + 2
                for h in range(8):
                    E("dve", lambda e, h=h, nkb=nkb: e.tensor_scalar(out=biasT3[:, h, 0:nkb], in0=cumT3[:, 0:nkb, h], scalar1=-1.0, scalar2=bcs[:, h:h + 1],
                                                            op0=ALU.mult, op1=ALU.add), reads=["cumT", "bcs"], writes=["biasT"])

                dump(P, "xr", R1[:, 0:2072], [("xr", j) for j in range(8)] + [("xrh", j) for j in range(8)])
                dump(P, "gyB", R1[:, 2072:4120], [("gy", j) for j in range(8)])
                dump(P, "qT", R1[:, 6168:7192].bitcast(BF16), [("qT", h) for h in range(8)])
                dump(P, "KT", KT3[:, :, 0:T], [("KT", h) for h in range(8)])
                dump(P, "Vt", Vt[:, 0:2 * 8 * 130], ["Vt"])
                dump(P, "cumr", cumr[:], ["cumr1"], np_=8)
                dump(P, "cumT", cumT[:], ["cumT"])
                dump(P, "biasT", biasT[:], ["biasT"])
                for j in range(8):
                    cw = lambda k, j=j: par[:, 32 + j * 4 + k:33 + j * 4 + k]
                    u_, r_, i_, a_, m_, bb_, hh_ = (tA[:, k, :] for k in range(7))
                    E("dve", lambda e, j=j, cw=cw: e.tensor_scalar(out=tA[:, 0, :], in0=xr[:, j, 3:3 + T], scalar1=cw(3), scalar2=par[:, 64 + j:65 + j],
                                                                   op0=ALU.mult, op1=ALU.add), reads=[("xr", j), ("xrh", j), "small"], writes=["u"])
                    for k in range(3):
                        E("dve", lambda e, j=j, k=k, cw=cw: e.scalar_tensor_tensor(out=tA[:, 0, :], in0=xr[:, j, k:k + T], scalar=cw(k), in1=tA[:, 0, :],
                                                                                   op0=ALU.mult, op1=ALU.add), reads=[("xr", j), ("xrh", j), "u"], writes=["u"])
                    E("pool", lambda e: e.tensor_copy(out=ubf[:], in_=tA[:, 0, :]), reads=["u"], writes=["ubf"])
                    if j == 7:
                        pass
                    E("pe", lambda e, j=j: e.matmul(ps[4][:, 0:T], lhsT=wgb[:, j * 128:(j + 1) * 128], rhs=ubf[:], start=True, stop=True),
                      reads=["ubf", "wgb"], writes=[("ps", 4)])
                    E("pe", lambda e, j=j: e.matmul(ps[5][:, 0:T], lhsT=wgb[:, 1024 + j * 128:1024 + (j + 1) * 128], rhs=ubf[:], start=True, stop=True),
                      reads=["ubf", "wgb"], writes=[("ps", 5)])
                    sigmoid_from(tA[:, 1, :], ps[4][:, 0:T], tA[:, 1, :], [("ps", 4), "der"], "r", neg_bias=der[:, 16 + j:17 + j])
                    sigmoid_from(tA[:, 2, :], ps[5][:, 0:T], tA[:, 2, :], [("ps", 5), "der"], "ig", neg_bias=der[:, 24 + j:25 + j])
                    E("act", lambda e, j=j: e.activation(out=tA[:, 3, :], in_=tA[:, 1, :], func=AF.Exp, scale=der[:, j:j + 1]), reads=["r", "der"], writes=["a"])
                    E("act", lambda e, j=j: e.activation(out=tA[:, 4, :], in_=tA[:, 1, :], func=AF.Exp, scale=der[:, 8 + j:9 + j]), reads=["r", "der"], writes=["m"])
                    E("act", lambda e: e.activation(out=tA[:, 4, :], in_=tA[:, 4, :], func=AF.Ln, scale=-1.0, bias=1.0), reads=["m"], writes=["m"])
                    E("act", lambda e: e.activation(out=tA[:, 4, :], in_=tA[:, 4, :], func=AF.Exp, scale=0.5), reads=["m"], writes=["m"])
                    E("pool", lambda e: e.tensor_tensor(out=tA[:, 5, :], in0=tA[:, 4, :], in1=tA[:, 2, :], op=ALU.mult), reads=["m", "ig"], writes=["bb"])
                    E("pool", lambda e: e.tensor_tensor(out=tA[:, 5, :], in0=tA[:, 5, :], in1=tA[:, 0, :], op=ALU.mult), reads=["bb", "u"], writes=["bb"])
                    if i == 0:
                        E("dve", lambda e: e.tensor_tensor_scan(out=tA[:, 6, :], data0=tA[:, 3, :], data1=tA[:, 5, :], initial=0.0,
                                                                op0=ALU.mult, op1=ALU.add), reads=["a", "bb"], writes=["hh"])
                    else:
                        E("dve", lambda e, j=j: e.tensor_tensor_scan(out=tA[:, 6, :], data0=tA[:, 3, :], data1=tA[:, 5, :], initial=hcar[:, j:j + 1],
                                                                     op0=ALU.mult, op1=ALU.add), reads=["a", "bb", "hcar"], writes=["hh"])
                    E("dve", lambda e, j=j: e.tensor_copy(out=hcar[:, j:j + 1], in_=tA[:, 6, T - 1:T]), reads=["hh"], writes=["hcar"])
                    E("dve", lambda e, j=j: e.tensor_tensor(out=gy[:, j, :], in0=tA[:, 6, :], in1=gy[:, j, :], op=ALU.mult),
                      reads=["hh", ("gy", j)], writes=[("gy", j)])
                    E("act", lambda e, j=j: e.activation(out=sqb[:, T:2 * T], in_=gy[:, j, :], func=AF.Square), reads=[("gy", j)], writes=["sqb2"])
                    E("pe", lambda e, j=j: e.matmul(ps[3][:, 0:T], lhsT=onesb[:], rhs=sqb[:, T:2 * T], start=(j == 0), stop=(j == 7)),
                      reads=["sqb2", "onesb"], writes=[("ps", 3)])
                    E("pool", lambda e, j=j: e.tensor_copy(out=halo3[:, j, :], in_=xr[:, j, T:T + 3]), reads=[("xr", j), "R1ph"], writes=[("halo", j)])
                E("act", lambda e: e.activation(out=rrt[:, T:2 * T], in_=ps[3][:, 0:T], func=AF.Ln, scale=1.0 / 1024, bias=EPS), reads=[("ps", 3)], writes=["rr2"])
                E("act", lambda e: e.activation(out=rrt[:, T:2 * T], in_=rrt[:, T:2 * T], func=AF.Exp, scale=-0.5), reads=["rr2"], writes=["rr2"])
                for j in range(8):
                    E("dve", lambda e, j=j: e.scalar_tensor_tensor(out=hT3[:, j, :], in0=gy[:, j, :], scalar=par[:, 96 + j:97 + j], in1=rrt[:, T:2 * T],
                                                                   op0=ALU.mult, op1=ALU.mult), reads=[("gy", j), "rr2", "small"], writes=["hT"])

                dump(P, "yrec", R1[:, 2072:4120], [("gy", j) for j in range(8)])
                dump(P, "hTrec", hT[:, 0:8 * T], ["hT"])
                for h in range(8):
                    ob = 4 + 2 * (h % 2)
                    for kb in range(nkb):
                        c0 = 128 if kb == nkb - 1 else 0
                        sb_ = kb % 2
                        pslot = (h * 32 + kb) % 3
                        E("pe", lambda e, h=h, kb=kb, c0=c0, sb_=sb_: e.matmul(ps[sb_][:, c0:T], lhsT=KT3[:, h, kb * 128:(kb + 1) * 128],
                                                                               rhs=qT[:, h, c0:T], start=True, stop=True),
                          reads=[("KT", h), ("qT", h)], writes=[("ps", sb_)])
                        E("act", lambda e, h=h, kb=kb, c0=c0, sb_=sb_, pslot=pslot: e.activation(out=PT3[:, pslot, c0:T], in_=ps[sb_][:, c0:T], func=AF.Exp,
                                                                                                 bias=biasT3[:, h, kb:kb + 1]),
                          reads=[("ps", sb_), "biasT"], writes=[("PT", pslot)])
                        if kb >= nkb - 2:
                            dc = 0 if kb == nkb - 2 else 128
                            E("pool", lambda e, pslot=pslot, dc=dc: e.tensor_tensor(out=PT3[:, pslot, dc:dc + 128], in0=PT3[:, pslot, dc:dc + 128],
                                                                                    in1=trib[:], op=ALU.mult),
                              reads=[("PT", pslot), "trib"], writes=[("PT", pslot)])
                        for s in range(2):
                            if s * 128 < c0:
                                continue
                            lastkb = nkb - 2 + s
                            E("pe", lambda e, h=h, kb=kb, s=s, pslot=pslot, ob=ob, lastkb=lastkb: e.matmul(
                                ps[ob + s][:, 0:129], lhsT=PT3[:, pslot, s * 128:(s + 1) * 128], rhs=Vt4[:, kb, h, 0:129],
                                start=(kb == 0), stop=(kb == lastkb)),
                              reads=[("PT", pslot), "Vt"], writes=[("ps", ob + s)])
                    for s in range(2):
                        E("dve", lambda e, s=s, ob=ob, h=h: e.reciprocal(out=stat[:, 8 + 2 * (h % 2) + s:9 + 2 * (h % 2) + s], in_=ps[ob + s][:, 128:129]),
                          reads=[("ps", ob + s)], writes=[("rl", h % 2, s)])
                        E("dve", lambda e, s=s, ob=ob, h=h: e.tensor_scalar(out=Ot[:, s, h * 128:(h + 1) * 128], in0=ps[ob + s][:, 0:128],
                                                                           scalar1=stat[:, 8 + 2 * (h % 2) + s:9 + 2 * (h % 2) + s], scalar2=None, op0=ALU.mult),
                          reads=[("ps", ob + s), ("rl", h % 2, s), "R1ph"], writes=[("Ot", s)])
                for s in range(2):
                    E("act", lambda e, s=s: e.activation(out=stg0[:, 0:1024], in_=Ot[:, s, :], func=AF.Square, accum_out=stat[:, 12 + s:13 + s]),
                      reads=[("Ot", s)], writes=[("stg", 0), ("st4", s)])
                    E("act", lambda e, s=s: e.activation(out=stat[:, 14 + s:15 + s], in_=stat[:, 12 + s:13 + s], func=AF.Ln, scale=1.0 / 1024, bias=EPS),
                      reads=[("st4", s)], writes=[("st5", s)])
                    E("act", lambda e, s=s: e.activation(out=stat[:, 16 + s:17 + s], in_=stat[:, 14 + s:15 + s], func=AF.Exp, scale=-0.5),
                      reads=[("st5", s)], writes=[("st6", s)])
                    E("act", lambda e, s=s: e.activation(out=xn[:, 0:1024], in_=Ot[:, s, :], func=AF.Copy, scale=stat[:, 16 + s:17 + s]),
                      reads=[("Ot", s), ("st6", s)], writes=["xn"])
                    for h in range(8):
                        E("pe", lambda e, h=h: e.transpose(out=psb[6][:, h * 128:(h + 1) * 128], in_=xn[:, h * 128:(h + 1) * 128], identity=idb[:]),
                          reads=["xn", "idb"] if h in (0, 7) else (), writes=[("ps", 6)] if h in (0, 7) else (), mark=(h == 7))
                    for h in range(8):
                        E("dve", lambda e, h=h, s=s: e.tensor_scalar(out=hT3[:, 8 + h, s * 128:(s + 1) * 128], in0=psb[6][:, h * 128:(h + 1) * 128],
                                                                    scalar1=par[:, 104 + h:105 + h], scalar2=None, op0=ALU.mult),
                          reads=[("ps", 6), "small"], writes=["hT"])

                dump(P, "Ot", R1[:, 4120:6168], [("Ot", 0), ("Ot", 1)])
                dump(P, "mixT", hT[:], ["hT"])
                def proj_tok(off, nkg, src3, v, skey):
                    for ng in range(4):
                        for kg in range(nkg):
                            sl = wload(off + ng * nkg + kg)
                            for cc in range(4):
                                c = kg * 4 + cc
                                for s in range(2):
                                    first = (kg == 0 and cc == 0)
                                    last = (kg == nkg - 1 and cc == 3)
                                    bk = 2 * (ng % 2) + s
                                    E("pe", lambda e, sl=sl, cc=cc, c=c, s=s, first=first, last=last, bk=bk: e.matmul(
                                        ps[bk][:, :], lhsT=src3[:, c, s * 128:(s + 1) * 128], rhs=Wslot[sl][:, cc * 512:(cc + 1) * 512],
                                        start=first, stop=last),
                                      reads=[("W", sl), skey, "R1ph"] if cc in (0, 3) else (), writes=[("ps", bk)] if (first or last) else (),
                                      mark=(cc == 3))
                        for s in range(2):
                            bk = 2 * (ng % 2) + s
                            E("dve", lambda e, s=s, bk=bk, ng=ng: e.tensor_tensor(out=tE[:, s, :], in0=ps[bk][:, :], in1=gt3[:, v, ng * 512:(ng + 1) * 512],
                                                                                  op=ALU.mult), reads=[("ps", bk), ("gt", v)], writes=[("tE", s)])
                            E("pool", lambda e, s=s, ng=ng: e.tensor_tensor(out=xt3[:, s, ng * 512:(ng + 1) * 512], in0=xt3[:, s, ng * 512:(ng + 1) * 512],
                                                                            in1=tE[:, s, :], op=ALU.add), reads=[("tE", s), ("xt", s)], writes=[("xt", s)])

                proj_tok(OFF_OUT, 4, hT3, 0, "hT")
                dump(P, "x1", xt[:], [("xt", 0), ("xt", 1)])
                norm_to_hT(b, 1, ())
                dump(P, "h2T", hT[:], ["hT"])
                for j in range(NFF):
                    slg = wload(OFF_UP + 2 * j)
                    slu = wload(OFF_UP + 2 * j + 1)
                    bg = 4 * (j % 2)
                    bu = bg + 1
                    for (sl, bk) in ((slg, bg), (slu, bu)):
                        for c in range(16):
                            E("pe", lambda e, sl=sl, c=c, bk=bk: e.matmul(ps[bk][:, 0:T], lhsT=Wslot[sl][:, c * 128:(c + 1) * 128], rhs=hT3[:, c, :],
                                                                          start=(c == 0), stop=(c == 15)),
                              reads=[("W", sl), "hT"] if c in (0, 15) else (), writes=[("ps", bk)] if c in (0, 15) else (), mark=(c == 15))
                    t1 = tA[:, 10 + (j % 2), :]
                    sigmoid_from(t1, ps[bg][:, 0:T], t1, [("ps", bg)], "sg%d" % (j % 2))
                    E("dve", lambda e, t1=t1, bg=bg: e.tensor_tensor(out=t1, in0=t1, in1=ps[bg][:, 0:T], op=ALU.mult),
                      reads=["sg%d" % (j % 2), ("ps", bg)], writes=["sg%d" % (j % 2)])
                    first = ["R1ph"] if j == 0 else []
                    E("dve", lambda e, t1=t1, bu=bu, j=j: e.tensor_tensor(out=actT[:, j, :], in0=t1, in1=ps[bu][:, 0:T], op=ALU.mult),
                      reads=["sg%d" % (j % 2), ("ps", bu)] + ([] if j == 0 else ["R1ph"]), writes=["actT"] + first)
                dump(P, "actT", R1[:, 0:5632].bitcast(BF16), ["actT"])
                proj_tok(OFF_DN, 11, actT, 1, "actT")
                P.dma("pool", "xout", [(out_d[b, t0 + s * 128:t0 + (s + 1) * 128, :], xt3[:, s, :]) for s in range(2)],
                      reads=[("xt", 0), ("xt", 1)], writes=[("out", b, i)])
        fin = [v for k, v in P.last_w.items() if isinstance(k, tuple) and k[0] in ("out", "dbgout")]
        P.wait_all("pool", fin)
        P.run()
    P.close()
    return nc


_NC_CACHE = {}


def _stat_blocks(W):
    K, N = W.shape
    nch = N // 128
    return np.ascontiguousarray(W.reshape(16, 128, nch, 128).transpose(2, 1, 0, 3)).reshape(nch, 128, 2048)


def _mov_blocks(W):
    K, N = W.shape
    nkg = K // 512
    NG = N // 512
    return np.ascontiguousarray(W.reshape(nkg, 4, 128, NG, 512).transpose(3, 0, 2, 1, 4)).reshape(NG * nkg, 128, 2048)


def _fm(v):
    v = np.asarray(v, np.float32).reshape(-1, 128)
    return np.ascontiguousarray(v.T)


def kernel(x, c, w_ada, b_ada, g_mix, w_in, conv_w, conv_b, w_gate_a, b_gate_a, w_gate_x, b_gate_x,
           lru_logit, b_forget, g_q, g_k, g_out_rec, g_out_att, w_out, g_ffn, w_up, w_down):
    f = lambda a: np.asarray(a, np.float32)
    x = f(x); c = f(c)
    w_ada = f(w_ada)[0]; w_in = f(w_in)[0]; w_out = f(w_out)[0]; w_up = f(w_up)[0]; w_down = f(w_down)[0]
    gate_b = _stat_blocks(w_up[:, :DFF])
    up_b = _stat_blocks(w_up[:, DFF:])
    wsrc = np.concatenate([
        _stat_blocks(w_ada),
        _stat_blocks(w_in[:, 0:4096]),
        _mov_blocks(w_in[:, 4096:5120]),
        _mov_blocks(w_out),
        np.stack([gate_b, up_b], 1).reshape(2 * NFF, 128, 2048),
        _mov_blocks(w_down),
    ], 0)
    assert wsrc.shape[0] == N_ADA + N_CONV
    par = np.zeros((128, 160), np.float32)
    par[:, 0:16] = _fm(f(g_mix)[0])
    par[:, 16:32] = _fm(f(g_ffn)[0])
    par[:, 32:64] = f(conv_w)[0].reshape(4, 8, 128).transpose(2, 1, 0).reshape(128, 32)
    par[:, 64:72] = _fm(f(conv_b)[0])
    par[:, 72:80] = _fm(f(b_gate_a)[0])
    par[:, 80:88] = _fm(f(b_gate_x)[0])
    par[:, 88:96] = _fm(f(lru_logit)[0])
    par[:, 96:104] = _fm(f(g_out_rec)[0])
    par[:, 104:112] = _fm(f(g_out_att)[0])
    par[:, 112] = f(g_q)[0]
    par[:, 113] = f(g_k)[0]
    par[0:8, 114] = f(b_forget)[0]
    bada = _fm(f(b_ada)[0])
    wg = np.stack([f(w_gate_a)[0], f(w_gate_x)[0]], 0).transpose(2, 0, 1, 3).reshape(128, 2048)
    wf = w_in[:, 5120:5128].reshape(16, 128, 8).transpose(1, 0, 2).reshape(128, 128)
    con = np.concatenate([np.eye(128, dtype=np.float32), np.triu(np.ones((128, 128), np.float32)),
                          np.ones((128, 128), np.float32)], 1)
    n_cores = 8
    in_maps = []
    for ci in range(n_cores):
        c2 = c[2 * ci:2 * ci + 2]
        cT = c2.reshape(2, 16, 128).transpose(2, 1, 0).reshape(128, 32)
        small = np.ascontiguousarray(np.concatenate([par, bada, cT, wg, wf, con], 1).astype(np.float32))
        assert small.shape == (128, C_END)
        in_maps.append({"x": np.ascontiguousarray(x[2 * ci:2 * ci + 2]), "wsrc": wsrc, "small": small})
    if "nc" not in _NC_CACHE:
        _NC_CACHE["nc"] = build_nc()
    res = run_bass_kernel_spmd(_NC_CACHE["nc"], in_maps, core_ids=list(range(n_cores)))
    return np.concatenate([np.asarray(r["out"], np.float32) for r in res.results], 0)
```

```python
import numpy as np
import concourse.bass as bass
import concourse.mybir as mybir
from concourse.bass_utils import run_bass_kernel_spmd

F32 = mybir.dt.float32
BF16 = mybir.dt.bfloat16
AF = mybir.ActivationFunctionType
ALU = mybir.AluOpType

ENGS = ("pe", "act", "dve", "pool", "sp")

D = 2048
S = 2048
T = 256
NT = S // T
NSEQ = 2
DFF = 5632
NFF = DFF // 128
EPS = 1e-6
N_ADA = 96
N_CONV = 188
OFF_INS, OFF_INV, OFF_OUT, OFF_UP, OFF_DN = 0, 32, 40, 56, 144
C_PAR, C_BADA, C_CT, C_WG, C_WF, C_CON, C_END = 0, 160, 256, 288, 2336, 2464, 2848


class Prog:
    def __init__(self, nc):
        self.nc = nc
        self.q = {e: [] for e in ENGS}
        self.cnt = {e: 0 for e in ENGS}
        self.sems = {}
        self.seen = {e: {} for e in ENGS}
        self.last_w = {}
        self.readers = {}
        self.slot_cnt = {}
        self.ctx = []

    def sem(self, name):
        if name not in self.sems:
            cm = self.nc.semaphore(name)
            self.ctx.append(cm)
            self.sems[name] = cm.__enter__()
        return self.sems[name]

    def close(self):
        for cm in reversed(self.ctx):
            cm.__exit__(None, None, None)

    def _deps(self, eng, reads, writes, extra):
        deps = []
        for k in reads:
            t = self.last_w.get(k)
            if t is not None:
                deps.append(t)
        for k in writes:
            t = self.last_w.get(k)
            if t is not None and t[2] != eng:
                deps.append(t)
            for r in self.readers.get(k, ()):
                if r[2] != eng:
                    deps.append(r)
        deps.extend(extra)
        return deps

    def _commit(self, tok, reads, writes):
        for k in writes:
            self.last_w[k] = tok
            self.readers[k] = []
        for k in reads:
            self.readers.setdefault(k, []).append(tok)

    def _waits(self, eng, deps):
        best = {}
        for d in deps:
            if d is None:
                continue
            name, val, _ = d
            if self.seen[eng].get(name, 0) >= val:
                continue
            if best.get(name, 0) < val:
                best[name] = val
        for name, val in best.items():
            self.seen[eng][name] = val
        return list(best.items())

    def emit(self, eng, fn, reads=(), writes=(), extra=(), mark=True):
        deps = self._deps(eng, reads, writes, extra)
        waits = self._waits(eng, deps)
        tok = None
        if mark:
            self.cnt[eng] += 1
            tok = ("e_" + eng, self.cnt[eng], eng)
        self.q[eng].append((waits, fn, ("e_" + eng, 1) if mark else None))
        if mark:
            self._commit(tok, reads, writes)
        return tok

    def dma(self, eng, slot, pairs, reads=(), writes=(), extra=()):
        deps = self._deps("dma_none", reads, writes, extra)
        waits = self._waits(eng, deps)
        name = "d_" + slot
        self.sem(name)
        for i, (o, a) in enumerate(pairs):
            def fn(e, o=o, a=a):
                return e.dma_start(out=o, in_=a)
            self.q[eng].append((waits if i == 0 else [], fn, (name, 16)))
        self.slot_cnt[name] = self.slot_cnt.get(name, 0) + 16 * len(pairs)
        tok = (name, self.slot_cnt[name], "dma")
        self._commit(tok, reads, writes)
        return tok

    def wait_all(self, eng, toks):
        waits = self._waits(eng, toks)
        self.q[eng].append((waits, None, None))

    def run(self):
        nc = self.nc
        for e in ENGS:
            self.sem("e_" + e)
        sems = self.sems
        q = self.q

        def play(engine, lst):
            for waits, fn, inc in lst:
                for name, val in waits:
                    engine.wait_ge(sems[name], val)
                if fn is None:
                    continue
                ins = fn(engine)
                if inc is not None:
                    ins.then_inc(sems[inc[0]], inc[1])

        with nc.Block() as block:
            @block.sync
            def _(e):
                play(e, q["sp"])

            @block.tensor
            def _(e):
                play(e, q["pe"])

            @block.scalar
            def _(e):
                play(e, q["act"])

            @block.vector
            def _(e):
                play(e, q["dve"])

            @block.gpsimd
            def _(e):
                play(e, q["pool"])


DBG = None


def build_nc():
    nc = bass.Bass("TRN2", target_bir_lowering=False)
    dbg = DBG
    if dbg:
        d32 = nc.dram_tensor("d32", [128, 32768], F32, kind="ExternalOutput").ap()
        d16 = nc.dram_tensor("d16", [128, 65536], BF16, kind="ExternalOutput").ap()
    dpos = {"32": 0, "16": 0}
    dmap = {}

    def dump(P, name, ap, keys, np_=128):
        if not dbg:
            return
        is16 = ap.dtype == BF16
        kk = "16" if is16 else "32"
        n = 1
        for d_ in ap.shape[1:]:
            n *= d_
        o = dpos[kk]
        dpos[kk] += n
        dmap[name] = (kk, o, tuple(ap.shape))
        dst = (d16 if is16 else d32)[0:np_, o:o + n]
        if len(ap.shape) == 3:
            dst = dst.rearrange("p (a b) -> p a b", a=ap.shape[1])
        P.dma("pool", "dbg", [(dst, ap)], reads=keys, writes=[("dbgout", name)])
    build_nc.dmap = dmap

    x_d = nc.dram_tensor("x", [NSEQ, S, D], F32, kind="ExternalInput").ap()
    wsrc = nc.dram_tensor("wsrc", [N_ADA + N_CONV, 128, 2048], F32, kind="ExternalInput").ap()
    small_d = nc.dram_tensor("small", [128, C_END], F32, kind="ExternalInput").ap()
    out_d = nc.dram_tensor("out", [NSEQ, S, D], F32, kind="ExternalOutput").ap()
    wbf = nc.dram_tensor("wbf", [N_CONV, 128, 2048], BF16, kind="Internal").ap()

    P = Prog(nc)
    E = P.emit
    import contextlib
    with contextlib.ExitStack() as es:
        def sb(name, shape, dt):
            return es.enter_context(nc.sbuf_tensor("sb_" + name, shape, dt))

        def pst(name):
            return es.enter_context(nc.psum_tensor(name, [128, 512], F32))

        small = sb("small", [128, C_END], F32)
        der = sb("der", [128, 64], F32)
        cact = sb("cact", [128, 32], F32)
        ctmp = sb("ctmp", [128, 64], F32)
        modT = sb("modT", [128, 192], F32)
        ABt = sb("ABt", [128, 64], F32)
        xt = sb("xt", [128, 2 * D], F32)
        hT = sb("hT", [128, 16 * T], BF16)
        xn = sb("xn", [128, D], BF16)
        KT = sb("KT", [128, 8 * S], BF16)
        Vt = sb("Vt", [128, 16 * 8 * 130], BF16)
        R1 = sb("R1", [128, 7192], F32)
        tmpA = sb("tmpA", [128, 12 * T], F32)
        ubf = sb("ubf", [128, T], BF16)
        sqb = sb("sqb", [128, 2 * T], BF16)
        sqq = sb("sqq", [128, 2 * T], BF16)
        rrq = sb("rrq", [128, 2 * T], F32)
        rrt = sb("rrt", [128, 2 * T], F32)
        PT = sb("PT", [128, 3 * T], BF16)
        Wr = sb("Wr", [128, 4 * 2048], BF16)
        stg0 = sb("stg0", [128, 2048], F32)
        gtbc = sb("gtbc", [128, 2 * D], F32)
        tmpE = sb("tmpE", [128, 2 * 512], F32)
        Gt = sb("Gt", [128, 2 * 128], F32)
        wgb = sb("wgb", [128, 2048], BF16)
        wfb = sb("wfb", [128, 128], BF16)
        idb = sb("idb", [128, 128], BF16)
        trib = sb("trib", [128, 128], BF16)
        onesb = sb("onesb", [128, 128], BF16)
        cumT = sb("cumT", [128, 16 * 8], F32)
        biasT = sb("biasT", [128, 8 * 16], F32)
        stat = sb("stat", [128, 32], F32)
        cumr = sb("cumr", [8, 2 * T], F32)
        hcar = sb("hcar", [128, 8], F32)
        ccar = sb("ccar", [8, 2], F32)
        bcs = sb("bcs", [128, 16], F32)
        o256 = sb("o256", [8, T], F32)
        halo = sb("halo", [128, 24], F32)
        ps = [pst("ps%d" % i) for i in range(8)]

        par = small[:, C_PAR:C_BADA]
        bada = small[:, C_BADA:C_CT]
        cTt = small[:, C_CT:C_WG]
        wg32 = small[:, C_WG:C_WF]
        wf32 = small[:, C_WF:C_CON]
        ident_f = small[:, C_CON:C_CON + 128]
        tri_f = small[:, C_CON + 128:C_CON + 256]
        ones_f = small[:, C_CON + 256:C_CON + 384]

        stg = [stg0[:], Wr[:, 0:4096].bitcast(F32), Wr[:, 4096:8192].bitcast(F32), xt[:, 0:2048], xt[:, 2048:4096]]
        cbf = [gtbc[:, i * 1024:(i + 1) * 1024].bitcast(BF16) for i in range(4)]
        Wslot = [Wr[:, i * 2048:(i + 1) * 2048] for i in range(4)]
        xr = R1[:, 0:2072].rearrange("p (j t) -> p j t", j=8)
        gy = R1[:, 2072:4120].rearrange("p (j t) -> p j t", j=8)
        Ot = R1[:, 4120:6168].rearrange("p (s f) -> p s f", s=2)
        qT = R1[:, 6168:7192].bitcast(BF16).rearrange("p (h t) -> p h t", h=8)
        actT = R1[:, 0:5632].bitcast(BF16).rearrange("p (j t) -> p j t", j=NFF)
        hT3 = hT[:].rearrange("p (c t) -> p c t", c=16)
        xt3 = xt[:].rearrange("p (s f) -> p s f", s=2)
        KT3 = KT[:].rearrange("p (h t) -> p h t", h=8)
        Vt4 = Vt[:].rearrange("p (b h d) -> p b h d", b=16, h=8)
        tA = tmpA[:].rearrange("p (k t) -> p k t", k=12)
        halo3 = halo[:].rearrange("p (j k) -> p j k", j=8)
        PT3 = PT[:].rearrange("p (k t) -> p k t", k=3)
        gt3 = gtbc[:].rearrange("p (v f) -> p v f", v=2)
        tE = tmpE[:].rearrange("p (k f) -> p k f", k=2)
        cumT3 = cumT[:].rearrange("p (b h) -> p b h", b=16)
        biasT3 = biasT[:].rearrange("p (h b) -> p h b", h=8)
        modT3 = modT[:].rearrange("p (c b) -> p c b", b=2)
        AB4 = ABt[:].rearrange("p (b w c) -> p b w c", b=2, w=2)
        psb = [p_[:].bitcast(BF16) for p_ in ps]

        P.dma("sp", "small", [(small[:], small_d[:, :])], writes=["small"])
        E("dve", lambda e: e.tensor_copy(out=idb[:], in_=ident_f), reads=["small"], writes=["idb"])
        E("dve", lambda e: e.tensor_copy(out=trib[:], in_=tri_f), reads=["small"], writes=["trib"])
        E("dve", lambda e: e.tensor_copy(out=onesb[:], in_=ones_f), reads=["small"], writes=["onesb"])
        E("dve", lambda e: e.tensor_copy(out=wgb[:], in_=wg32), reads=["small"], writes=["wgb"])
        E("dve", lambda e: e.tensor_copy(out=wfb[:], in_=wf32), reads=["small"], writes=["wfb"])
        E("pool", lambda e: e.memset(Vt[:], 1.0), writes=["Vt"])
        E("pool", lambda e: e.memset(o256[:], 1.0), writes=["ones256"])
        E("act", lambda e: e.activation(out=ctmp[:, 0:8], in_=par[:, 88:96], func=AF.Exp, scale=-1.0),
          reads=["small"], writes=["ctmp"])
        E("act", lambda e: e.activation(out=ctmp[:, 8:16], in_=ctmp[:, 0:8], func=AF.Ln, bias=1.0),
          reads=["ctmp"], writes=["ctmp2"])
        E("dve", lambda e: e.tensor_scalar(out=der[:, 0:8], in0=ctmp[:, 8:16], scalar1=-8.0, scalar2=None, op0=ALU.mult),
          reads=["ctmp2"], writes=["der"])
        E("dve", lambda e: e.tensor_scalar(out=der[:, 8:16], in0=ctmp[:, 8:16], scalar1=-16.0, scalar2=None, op0=ALU.mult),
          reads=["ctmp2"], writes=["der"])
        E("dve", lambda e: e.tensor_scalar(out=der[:, 16:32], in0=par[:, 72:88], scalar1=-1.0, scalar2=None, op0=ALU.mult),
          reads=["small"], writes=["der"])
        E("dve", lambda e: e.tensor_scalar(out=der[:, 32:33], in0=par[:, 112:113], scalar1=128.0 ** -0.5, scalar2=None, op0=ALU.mult),
          reads=["small"], writes=["der"])
        E("dve", lambda e: e.tensor_scalar(out=der[:, 33:34], in0=par[:, 114:115], scalar1=-1.0, scalar2=None, op0=ALU.mult),
          reads=["small"], writes=["der"])
        E("act", lambda e: e.activation(out=ctmp[:, 16:48], in_=cTt, func=AF.Exp, scale=-1.0), reads=["small"], writes=["ctmp3"])
        E("act", lambda e: e.activation(out=ctmp[:, 16:48], in_=ctmp[:, 16:48], func=AF.Ln, bias=1.0), reads=["ctmp3"], writes=["ctmp3"])
        E("act", lambda e: e.activation(out=ctmp[:, 16:48], in_=ctmp[:, 16:48], func=AF.Exp, scale=-1.0), reads=["ctmp3"], writes=["ctmp3"])
        E("dve", lambda e: e.tensor_tensor(out=cact[:], in0=ctmp[:, 16:48], in1=cTt, op=ALU.mult), reads=["ctmp3", "small"], writes=["cact"])

        def SK(sl):
            if sl in (1, 2):
                return [("stg", sl), ("W", 2 * sl - 2), ("W", 2 * sl - 1)]
            if sl in (3, 4):
                return [("stg", sl), ("xt", sl - 3)]
            return [("stg", sl)]

        def CK(cs):
            return [("cbf", cs), ("gt", 0), ("gt", 1)]

        order = []
        ia = 0
        for k in range(N_CONV):
            order.append(("c", k))
            if k % 2 == 1 and ia < N_ADA:
                order.append(("a", ia))
                ia += 1
        while ia < N_ADA:
            order.append(("a", ia))
            ia += 1
        pend = []
        for n, (kind, k) in enumerate(order):
            sl = n % 5
            if kind == "a":
                P.dma("sp", "stg%d" % sl, [(stg[sl], wsrc[k, :, :])], writes=SK(sl))
                for c in range(16):
                    E("pe", lambda e, sl=sl, c=c, nb=k: e.matmul(ps[0][:, 2 * nb:2 * nb + 2], lhsT=stg[sl][:, c * 128:(c + 1) * 128],
                                                                  rhs=cact[:, 2 * c:2 * c + 2], start=(c == 0), stop=(c == 15)),
                      reads=SK(sl) + ["cact"] if c in (0, 15) else (), writes=[("ps", 0)] if c == 15 else (), mark=(c == 15))
            else:
                cs = k % 4
                P.dma("sp", "stg%d" % sl, [(stg[sl], wsrc[N_ADA + k, :, :])], writes=SK(sl))
                if k % 2 == 0:
                    E("act", lambda e, sl=sl, cs=cs: e.activation(out=cbf[cs], in_=stg[sl], func=AF.Copy), reads=SK(sl), writes=CK(cs))
                else:
                    E("dve", lambda e, sl=sl, cs=cs: e.tensor_copy(out=cbf[cs], in_=stg[sl]), reads=SK(sl), writes=CK(cs))
                pend.append((k, cs))
                if len(pend) > 2:
                    k2, cs2 = pend.pop(0)
                    P.dma("sp", "cbf%d" % cs2, [(wbf[k2, :, :], cbf[cs2])], reads=CK(cs2), writes=[("wbf", k2)])
        for k2, cs2 in pend:
            P.dma("sp", "cbf%d" % cs2, [(wbf[k2, :, :], cbf[cs2])], reads=CK(cs2), writes=[("wbf", k2)])
        for b in range(2):
            E("dve", lambda e, b=b: e.tensor_tensor(out=modT3[:, :, b], in0=ps[0][:, 0:192].rearrange("p (c b) -> p c b", b=2)[:, :, b],
                                                    in1=bada, op=ALU.add), reads=[("ps", 0), "small"], writes=["modT"])
            E("dve", lambda e, b=b: e.scalar_tensor_tensor(out=AB4[:, b, 0, :], in0=modT3[:, 16:32, b], scalar=1.0, in1=par[:, 0:16],
                                                           op0=ALU.add, op1=ALU.mult), reads=["modT"], writes=["AB"])
            E("dve", lambda e, b=b: e.scalar_tensor_tensor(out=AB4[:, b, 1, :], in0=modT3[:, 64:80, b], scalar=1.0, in1=par[:, 16:32],
                                                           op0=ALU.add, op1=ALU.mult), reads=["modT"], writes=["AB"])

        wctr = [0]

        def wload(k):
            sl = wctr[0] % 4
            wctr[0] += 1
            P.dma("sp", "W%d" % sl, [(Wslot[sl], wbf[k, :, :])], reads=[("wbf", k)], writes=[("W", sl)])
            return sl

        psrot = [0]

        def nextbank(lo, n):
            b = lo + psrot[0] % n
            psrot[0] += 1
            return b

        def sigmoid_from(out_ap, in_ap, t1, rkeys, wkey, neg_bias=None, scale=1.0):
            kw = {} if neg_bias is None else {"bias": neg_bias}
            E("act", lambda e: e.activation(out=t1, in_=in_ap, func=AF.Exp, scale=-scale, **kw), reads=rkeys, writes=[wkey + "_t", wkey])
            E("act", lambda e: e.activation(out=t1, in_=t1, func=AF.Ln, bias=1.0), reads=[wkey + "_t"], writes=[wkey + "_t"])
            return E("act", lambda e: e.activation(out=out_ap, in_=t1, func=AF.Exp, scale=-1.0), reads=[wkey + "_t"], writes=[wkey])

        def norm_to_hT(b, which, first_extra_keys):
            Bcol0 = 0 if which == 0 else 48
            for s in range(2):
                E("act", lambda e, s=s: e.activation(out=stg0[:], in_=xt3[:, s, :], func=AF.Square, accum_out=stat[:, s:s + 1]),
                  reads=[("xt", s)], writes=[("stg", 0), ("stat", s)])
                E("act", lambda e, s=s: e.activation(out=stat[:, 2 + s:3 + s], in_=stat[:, s:s + 1], func=AF.Ln, scale=1.0 / D, bias=EPS),
                  reads=[("stat", s)], writes=[("stat2", s)])
                E("act", lambda e, s=s: e.activation(out=stat[:, 4 + s:5 + s], in_=stat[:, 2 + s:3 + s], func=AF.Exp, scale=-0.5),
                  reads=[("stat2", s)], writes=[("stat3", s)])
                E("act", lambda e, s=s: e.activation(out=xn[:], in_=xt3[:, s, :], func=AF.Copy, scale=stat[:, 4 + s:5 + s]),
                  reads=[("xt", s), ("stat3", s)], writes=["xn"])
                for g in range(2):
                    bk = 6 + g
                    for c8 in range(8):
                        c = g * 8 + c8
                        E("pe", lambda e, c=c, c8=c8, bk=bk: e.transpose(out=psb[bk][:, c8 * 128:(c8 + 1) * 128],
                                                                         in_=xn[:, c * 128:(c + 1) * 128], identity=idb[:]),
                          reads=["xn", "idb"] if c8 in (0, 7) else (), writes=[("ps", bk)] if c8 in (0, 7) else (), mark=(c8 == 7))
                    for c8 in range(8):
                        c = g * 8 + c8
                        E("dve", lambda e, c=c, c8=c8, bk=bk, s=s: e.tensor_scalar(
                            out=hT3[:, c, s * 128:(s + 1) * 128], in0=psb[bk][:, c8 * 128:(c8 + 1) * 128],
                            scalar1=AB4[:, b, which, c:c + 1], scalar2=modT3[:, Bcol0 + c, b:b + 1], op0=ALU.mult, op1=ALU.add),
                          reads=[("ps", bk), "AB", "modT"], writes=["hT"])

        dump(P, "modT", modT[:], ["modT"])
        dump(P, "AB", ABt[:], ["AB"])
        dump(P, "der", der[:], ["der"])
        for b in range(1 if dbg else NSEQ):
            for v in range(2):
                ch0 = 32 if v == 0 else 80
                for cc in range(16):
                    gsl = cc % 2
                    E("dve", lambda e, gsl=gsl, ch=ch0 + cc, b=b: e.tensor_scalar(out=Gt[:, gsl * 128:(gsl + 1) * 128], in0=ones_f,
                                                                           scalar1=modT3[:, ch, b:b + 1], scalar2=None, op0=ALU.mult),
                      reads=["modT", "small"], writes=[("Gt", gsl)])
                    E("pe", lambda e, gsl=gsl, cc=cc: e.matmul(ps[5][:, (cc % 4) * 128:(cc % 4 + 1) * 128], lhsT=Gt[:, gsl * 128:(gsl + 1) * 128],
                                                               rhs=ident_f, start=True, stop=True),
                      reads=[("Gt", gsl), "small"], writes=[("ps", 5)])
                    if cc % 4 == 3:
                        q4 = cc // 4
                        E("act", lambda e, v=v, q4=q4: e.activation(out=gt3[:, v, q4 * 512:(q4 + 1) * 512], in_=ps[5][:], func=AF.Copy),
                          reads=[("ps", 5)], writes=[("gt", v)])
            dump(P, "gt", gtbc[:], [("gt", 0), ("gt", 1)])
            for i in range(1 if dbg else NT):
                t0 = i * T
                P.dma("pool", "xin", [(xt3[:, s, :], x_d[b, t0 + s * 128:t0 + (s + 1) * 128, :]) for s in range(2)],
                      writes=[("xt", 0), ("xt", 1)])
                norm_to_hT(b, 0, ())
                dump(P, "hT1", hT[:], ["hT"])
                qk_pending = [None]
                for j in range(32):
                    sl = wload(OFF_INS + j)
                    bk = nextbank(0, 4)
                    for c in range(16):
                        E("pe", lambda e, sl=sl, c=c, bk=bk: e.matmul(ps[bk][:, 0:T], lhsT=Wslot[sl][:, c * 128:(c + 1) * 128],
                                                                      rhs=hT3[:, c, :], start=(c == 0), stop=(c == 15)),
                          reads=[("W", sl), "hT"] if c in (0, 15) else (), writes=[("ps", bk)] if c in (0, 15) else (), mark=(c == 15))
                    pj = ps[bk][:, 0:T]
                    if j < 8:
                        first = ["R1ph"] if j == 0 else []
                        E("act", lambda e, j=j, pj=pj: e.activation(out=xr[:, j, 3:3 + T], in_=pj, func=AF.Copy),
                          reads=[("ps", bk)] + ([] if j == 0 else ["R1ph"]), writes=[("xr", j)] + first)
                        if i == 0:
                            E("pool", lambda e, j=j: e.memset(xr[:, j, 0:3], 0.0), reads=[("xr", j)], writes=[("xrh", j)])
                        else:
                            E("pool", lambda e, j=j: e.tensor_copy(out=xr[:, j, 0:3], in_=halo3[:, j, :]), reads=[("xr", j), ("halo", j)], writes=[("xrh", j)])
                    elif j < 16:
                        jj = j - 8
                        E("act", lambda e, pj=pj: e.activation(out=tA[:, 8, :], in_=pj, func=AF.Square), reads=[("ps", bk), "R1ph"], writes=["g1"])
                        E("dve", lambda e, pj=pj: e.scalar_tensor_tensor(out=tA[:, 8, :], in0=tA[:, 8, :], scalar=1.0 / 0.044715, in1=pj,
                                                                         op0=ALU.add, op1=ALU.mult), reads=["g1", ("ps", bk)], writes=["g2"])
                        sigmoid_from(tA[:, 9, :], tA[:, 8, :], tA[:, 9, :], ["g2"], "g3", scale=1.5957691216 * 0.044715)
                        E("dve", lambda e, jj=jj, pj=pj: e.tensor_tensor(out=gy[:, jj, :], in0=tA[:, 9, :], in1=pj, op=ALU.mult),
                          reads=["g3", ("ps", bk), "R1ph"], writes=[("gy", jj)])
                    else:
                        h = (j - 16) % 8
                        isq = j < 24
                        par_ = j % 2
                        E("act", lambda e, pj=pj, par_=par_: e.activation(out=sqq[:, par_ * T:(par_ + 1) * T], in_=pj, func=AF.Square),
                          reads=[("ps", bk)], writes=[("sqq", par_)])

                        def finish(h=h, isq=isq, par_=par_, pj=pj, bk=bk, t0=t0):
                            nb_ = 4 + par_
                            E("pe", lambda e: e.matmul(ps[nb_][:, 0:T], lhsT=onesb[:], rhs=sqq[:, par_ * T:(par_ + 1) * T], start=True, stop=True),
                              reads=[("sqq", par_), "onesb"], writes=[("ps", nb_)])
                            E("act", lambda e: e.activation(out=rrq[:, par_ * T:(par_ + 1) * T], in_=ps[nb_][:, 0:T], func=AF.Ln, scale=1.0 / 128, bias=EPS),
                              reads=[("ps", nb_)], writes=[("rrq", par_)])
                            E("act", lambda e: e.activation(out=rrq[:, par_ * T:(par_ + 1) * T], in_=rrq[:, par_ * T:(par_ + 1) * T], func=AF.Exp, scale=-0.5),
                              reads=[("rrq", par_)], writes=[("rrq", par_)])
                            if isq:
                                E("dve", lambda e: e.scalar_tensor_tensor(out=qT[:, h, :], in0=pj, scalar=der[:, 32:33], in1=rrq[:, par_ * T:(par_ + 1) * T],
                                                                          op0=ALU.mult, op1=ALU.mult),
                                  reads=[("ps", bk), ("rrq", par_), "der", "R1ph"], writes=[("qT", h)])
                            else:
                                E("dve", lambda e: e.scalar_tensor_tensor(out=KT3[:, h, t0:t0 + T], in0=pj, scalar=par[:, 113:114], in1=rrq[:, par_ * T:(par_ + 1) * T],
                                                                          op0=ALU.mult, op1=ALU.mult),
                                  reads=[("ps", bk), ("rrq", par_), "small"], writes=[("KT", h)])
                        if qk_pending[0] is not None:
                            qk_pending[0]()
                        qk_pending[0] = finish
                if qk_pending[0] is not None:
                    qk_pending[0]()
                    qk_pending[0] = None
                for ng in range(2):
                    for kg in range(4):
                        sl = wload(OFF_INV + ng * 4 + kg)
                        for cc in range(4):
                            c = kg * 4 + cc
                            for s in range(2):
                                first = (kg == 0 and cc == 0)
                                last = (kg == 3 and cc == 3)
                                E("pe", lambda e, sl=sl, cc=cc, c=c, s=s, first=first, last=last: e.matmul(
                                    ps[s][:, :], lhsT=hT3[:, c, s * 128:(s + 1) * 128], rhs=Wslot[sl][:, cc * 512:(cc + 1) * 512],
                                    start=first, stop=last),
                                  reads=[("W", sl), "hT"] if cc in (0, 3) else (), writes=[("ps", s)] if (first or last) else (),
                                  mark=(cc == 3))
                    for s in range(2):
                        blk = t0 // 128 + s
                        E("act", lambda e, s=s, blk=blk, ng=ng: e.activation(
                            out=Vt4[:, blk, ng * 4:(ng + 1) * 4, 0:128], in_=ps[s][:, :].rearrange("p (h d) -> p h d", h=4), func=AF.Copy),
                          reads=[("ps", s)], writes=["Vt"])
                for c in range(16):
                    E("pe", lambda e, c=c: e.matmul(ps[2][0:8, 0:T], lhsT=wfb[:, c * 8:(c + 1) * 8], rhs=hT3[:, c, :], start=(c == 0), stop=(c == 15)),
                      reads=["wfb", "hT"] if c in (0, 15) else (), writes=[("ps", 2)] if c in (0, 15) else (), mark=(c == 15))
                E("act", lambda e: e.activation(out=cumr[:, 0:T], in_=ps[2][0:8, 0:T], func=AF.Exp, scale=-1.0, bias=der[0:8, 33:34]),
                  reads=[("ps", 2), "der"], writes=["cumr0"])
                E("act", lambda e: e.activation(out=cumr[:, 0:T], in_=cumr[:, 0:T], func=AF.Ln, bias=1.0), reads=["cumr0"], writes=["cumr0"])
                if i == 0:
                    E("dve", lambda e: e.tensor_tensor_scan(out=cumr[:, T:2 * T], data0=o256[:], data1=cumr[:, 0:T], initial=0.0,
                                                            op0=ALU.mult, op1=ALU.subtract),
                      reads=["cumr0", "ones256"], writes=["cumr1"])
                else:
                    E("dve", lambda e: e.tensor_tensor_scan(out=cumr[:, T:2 * T], data0=o256[:], data1=cumr[:, 0:T], initial=ccar[:, 0:1],
                                                            op0=ALU.mult, op1=ALU.subtract),
                      reads=["cumr0", "ccar", "ones256"], writes=["cumr1"])
                E("dve", lambda e: e.tensor_copy(out=ccar[:, 0:1], in_=cumr[:, 2 * T - 1:2 * T]), reads=["cumr1"], writes=["ccar"])
                for s in range(2):
                    blk = t0 // 128 + s
                    E("pe", lambda e, s=s: e.transpose(out=ps[3][:, s * 8:(s + 1) * 8], in_=cumr[:, T + s * 128:T + (s + 1) * 128], identity=ident_f[0:8, 0:8]),
                      reads=["cumr1", "small"], writes=[("ps", 3)])
                    E("dve", lambda e, s=s, blk=blk: e.tensor_copy(out=cumT3[:, blk, :], in_=ps[3][:, s * 8:(s + 1) * 8]),
                      reads=[("ps", 3)], writes=["cumT"])
                E("dve", lambda e: e.tensor_scalar(out=ctmp[0:8, 48:56], in0=ident_f[0:8, 0:8], scalar1=cumr[:, T + 127:T + 128], scalar2=None, op0=ALU.mult),
                  reads=["cumr1", "small"], writes=["dg"])
                E("pe", lambda e: e.matmul(ps[3][:, 16:24], lhsT=ones_f[0:8, :], rhs=ctmp[0:8, 48:56], start=True, stop=True),
                  reads=["dg", "small"], writes=[("ps", 3)])
                E("dve", lambda e: e.tensor_copy(out=bcs[:, 0:8], in_=ps[3][:, 16:24]), reads=[("ps", 3)], writes=["bcs"])
                nkb = t0 // 128 + 2
                for h in range(8):
                    E("dve", lambda e, h=h, nkb=nkb: e.tensor_scalar(out=biasT3[:, h, 0:nkb], in0=cumT3[:, 0:nkb, h], scalar1=-1.0, scalar2=bcs[:, h:h + 1],
                                                            op0=ALU.mult, op1=ALU.add), reads=["cumT", "bcs"], writes=["biasT"])

                dump(P, "xr", R1[:, 0:2072], [("xr", j) for j in range(8)] + [("xrh", j) for j in range(8)])
                dump(P, "gyB", R1[:, 2072:4120], [("gy", j) for j in range(8)])
                dump(P, "qT", R1[:, 6168:7192].bitcast(BF16), [("qT", h) for h in range(8)])
                dump(P, "KT", KT3[:, :, 0:T], [("KT", h) for h in range(8)])
                dump(P, "Vt", Vt[:, 0:2 * 8 * 130], ["Vt"])
                dump(P, "cumr", cumr[:], ["cumr1"], np_=8)
                dump(P, "cumT", cumT[:], ["cumT"])
                dump(P, "biasT", biasT[:], ["biasT"])
                deferred = []

                def rg_part1(j):
                    cw = lambda k: par[:, 32 + j * 4 + k:33 + j * 4 + k]
                    E("dve", lambda e: e.tensor_scalar(out=tA[:, 0, :], in0=xr[:, j, 3:3 + T], scalar1=cw(3), scalar2=par[:, 64 + j:65 + j],
                                                       op0=ALU.mult, op1=ALU.add), reads=[("xr", j), ("xrh", j), "small"], writes=["u"])
                    for k in range(3):
                        E("dve", lambda e, k=k: e.scalar_tensor_tensor(out=tA[:, 0, :], in0=xr[:, j, k:k + T], scalar=cw(k), in1=tA[:, 0, :],
                                                                       op0=ALU.mult, op1=ALU.add), reads=[("xr", j), ("xrh", j), "u"], writes=["u"])
                    E("pool", lambda e: e.tensor_copy(out=ubf[:], in_=tA[:, 0, :]), reads=["u"], writes=["ubf"])
                    E("pool", lambda e: e.tensor_copy(out=halo3[:, j, :], in_=xr[:, j, T:T + 3]), reads=[("xr", j), "R1ph"], writes=[("halo", j)])
                    E("pe", lambda e: e.matmul(ps[2][:, 0:T], lhsT=wgb[:, j * 128:(j + 1) * 128], rhs=ubf[:], start=True, stop=True),
                      reads=["ubf", "wgb"], writes=[("ps", 2)])
                    E("pe", lambda e: e.matmul(ps[2][:, T:2 * T], lhsT=wgb[:, 1024 + j * 128:1024 + (j + 1) * 128], rhs=ubf[:], start=True, stop=True),
                      reads=["ubf", "wgb"], writes=[("ps", 2)])

                def rg_part2a(j):
                    sigmoid_from(tA[:, 1, :], ps[2][:, 0:T], tA[:, 1, :], [("ps", 2), "der"], "r", neg_bias=der[:, 16 + j:17 + j])
                    sigmoid_from(tA[:, 2, :], ps[2][:, T:2 * T], tA[:, 2, :], [("ps", 2), "der"], "ig", neg_bias=der[:, 24 + j:25 + j])
                    E("act", lambda e: e.activation(out=tA[:, 3, :], in_=tA[:, 1, :], func=AF.Exp, scale=der[:, j:j + 1]), reads=["r", "der"], writes=["a"])
                    E("act", lambda e: e.activation(out=tA[:, 4, :], in_=tA[:, 1, :], func=AF.Exp, scale=der[:, 8 + j:9 + j]), reads=["r", "der"], writes=["m"])
                    E("act", lambda e: e.activation(out=tA[:, 4, :], in_=tA[:, 4, :], func=AF.Ln, scale=-1.0, bias=1.0), reads=["m"], writes=["m"])
                    E("act", lambda e: e.activation(out=tA[:, 4, :], in_=tA[:, 4, :], func=AF.Exp, scale=0.5), reads=["m"], writes=["m"])

                def rg_part2b(j, i=i):
                    E("pool", lambda e: e.tensor_tensor(out=tA[:, 5, :], in0=tA[:, 4, :], in1=tA[:, 2, :], op=ALU.mult), reads=["m", "ig"], writes=["bb"])
                    E("pool", lambda e: e.tensor_tensor(out=tA[:, 5, :], in0=tA[:, 5, :], in1=tA[:, 0, :], op=ALU.mult), reads=["bb", "u"], writes=["bb"])
                    if i == 0:
                        E("dve", lambda e: e.tensor_tensor_scan(out=tA[:, 6, :], data0=tA[:, 3, :], data1=tA[:, 5, :], initial=0.0,
                                                                op0=ALU.mult, op1=ALU.add), reads=["a", "bb"], writes=["hh"])
                    else:
                        E("dve", lambda e: e.tensor_tensor_scan(out=tA[:, 6, :], data0=tA[:, 3, :], data1=tA[:, 5, :], initial=hcar[:, j:j + 1],
                                                                op0=ALU.mult, op1=ALU.add), reads=["a", "bb", "hcar"], writes=["hh"])
                    E("dve", lambda e: e.tensor_copy(out=hcar[:, j:j + 1], in_=tA[:, 6, T - 1:T]), reads=["hh"], writes=["hcar"])
                    E("dve", lambda e: e.tensor_tensor(out=gy[:, j, :], in0=tA[:, 6, :], in1=gy[:, j, :], op=ALU.mult),
                      reads=["hh", ("gy", j)], writes=[("gy", j)])
                    pq = j % 2
                    E("act", lambda e: e.activation(out=sqb[:, pq * T:(pq + 1) * T], in_=gy[:, j, :], func=AF.Square), reads=[("gy", j)], writes=[("sqb2", pq)])
                    deferred.append(lambda: E("pe", lambda e: e.matmul(ps[3][:, 0:T], lhsT=onesb[:], rhs=sqb[:, pq * T:(pq + 1) * T], start=(j == 0), stop=(j == 7)),
                                              reads=[("sqb2", pq), "onesb"], writes=[("ps", 3)]))

                steps = [(h, kb) for h in range(8) for kb in range(nkb)]

                def at_qk(n):
                    h, kb = steps[n]
                    c0 = 128 if kb == nkb - 1 else 0
                    sb_ = n % 2
                    E("pe", lambda e: e.matmul(ps[sb_][:, c0:T], lhsT=KT3[:, h, kb * 128:(kb + 1) * 128], rhs=qT[:, h, c0:T], start=True, stop=True),
                      reads=[("KT", h), ("qT", h)], writes=[("ps", sb_)])

                def at_exp(n):
                    h, kb = steps[n]
                    c0 = 128 if kb == nkb - 1 else 0
                    sb_ = n % 2
                    pslot = n % 3
                    E("act", lambda e: e.activation(out=PT3[:, pslot, c0:T], in_=ps[sb_][:, c0:T], func=AF.Exp, bias=biasT3[:, h, kb:kb + 1]),
                      reads=[("ps", sb_), "biasT"], writes=[("PT", pslot)])
                    if kb >= nkb - 2:
                        dc = 0 if kb == nkb - 2 else 128
                        E("pool", lambda e: e.tensor_tensor(out=PT3[:, pslot, dc:dc + 128], in0=PT3[:, pslot, dc:dc + 128], in1=trib[:], op=ALU.mult),
                          reads=[("PT", pslot), "trib"], writes=[("PT", pslot)])

                def at_pv(n):
                    h, kb = steps[n]
                    c0 = 128 if kb == nkb - 1 else 0
                    pslot = n % 3
                    ob = 4 + 2 * (h % 2)
                    for s in range(2):
                        if s * 128 < c0:
                            continue
                        lastkb = nkb - 2 + s
                        E("pe", lambda e, s=s, lastkb=lastkb: e.matmul(ps[ob + s][:, 0:129], lhsT=PT3[:, pslot, s * 128:(s + 1) * 128], rhs=Vt4[:, kb, h, 0:129],
                                                                       start=(kb == 0), stop=(kb == lastkb)),
                          reads=[("PT", pslot), "Vt"], writes=[("ps", ob + s)])

                def at_evac(h):
                    ob = 4 + 2 * (h % 2)
                    for s in range(2):
                        col = 8 + 2 * (h % 2) + s
                        E("dve", lambda e, s=s, col=col: e.reciprocal(out=stat[:, col:col + 1], in_=ps[ob + s][:, 128:129]),
                          reads=[("ps", ob + s)], writes=[("rl", h % 2, s)])
                        E("dve", lambda e, s=s, col=col: e.tensor_scalar(out=Ot[:, s, h * 128:(h + 1) * 128], in0=ps[ob + s][:, 0:128],
                                                                        scalar1=stat[:, col:col + 1], scalar2=None, op0=ALU.mult),
                          reads=[("ps", ob + s), ("rl", h % 2, s), "R1ph"], writes=[("Ot", s)])

                rg_part1(0)
                at_qk(0)
                for n in range(len(steps)):
                    h, kb = steps[n]
                    if kb == 0:
                        rg_part2a(h)
                    if n + 1 < len(steps):
                        at_qk(n + 1)
                    at_exp(n)
                    at_pv(n)
                    if kb == nkb - 1:
                        rg_part2b(h)
                        if h + 1 < 8:
                            rg_part1(h + 1)
                        at_evac(h)
                        while len(deferred) > 1:
                            deferred.pop(0)()
                while deferred:
                    deferred.pop(0)()
                E("act", lambda e: e.activation(out=rrt[:, T:2 * T], in_=ps[3][:, 0:T], func=AF.Ln, scale=1.0 / 1024, bias=EPS), reads=[("ps", 3)], writes=["rr2"])
                E("act", lambda e: e.activation(out=rrt[:, T:2 * T], in_=rrt[:, T:2 * T], func=AF.Exp, scale=-0.5), reads=["rr2"], writes=["rr2"])
                for j in range(8):
                    E("dve", lambda e, j=j: e.scalar_tensor_tensor(out=hT3[:, j, :], in0=gy[:, j, :], scalar=par[:, 96 + j:97 + j], in1=rrt[:, T:2 * T],
                                                                   op0=ALU.mult, op1=ALU.mult), reads=[("gy", j), "rr2", "small"], writes=["hT"])
                dump(P, "yrec", R1[:, 2072:4120], [("gy", j) for j in range(8)])
                dump(P, "hTrec", hT[:, 0:8 * T], ["hT"])
                for s in range(2):
                    E("act", lambda e, s=s: e.activation(out=stg0[:, 0:1024], in_=Ot[:, s, :], func=AF.Square, accum_out=stat[:, 12 + s:13 + s]),
                      reads=[("Ot", s)], writes=[("stg", 0), ("st4", s)])
                    E("act", lambda e, s=s: e.activation(out=stat[:, 14 + s:15 + s], in_=stat[:, 12 + s:13 + s], func=AF.Ln, scale=1.0 / 1024, bias=EPS),
                      reads=[("st4", s)], writes=[("st5", s)])
                    E("act", lambda e, s=s: e.activation(out=stat[:, 16 + s:17 + s], in_=stat[:, 14 + s:15 + s], func=AF.Exp, scale=-0.5),
                      reads=[("st5", s)], writes=[("st6", s)])
                    E("act", lambda e, s=s: e.activation(out=xn[:, 0:1024], in_=Ot[:, s, :], func=AF.Copy, scale=stat[:, 16 + s:17 + s]),
                      reads=[("Ot", s), ("st6", s)], writes=["xn"])
                    for h in range(8):
                        E("pe", lambda e, h=h: e.transpose(out=psb[6][:, h * 128:(h + 1) * 128], in_=xn[:, h * 128:(h + 1) * 128], identity=idb[:]),
                          reads=["xn", "idb"] if h in (0, 7) else (), writes=[("ps", 6)] if h in (0, 7) else (), mark=(h == 7))
                    for h in range(8):
                        E("dve", lambda e, h=h, s=s: e.tensor_scalar(out=hT3[:, 8 + h, s * 128:(s + 1) * 128], in0=psb[6][:, h * 128:(h + 1) * 128],
                                                                    scalar1=par[:, 104 + h:105 + h], scalar2=None, op0=ALU.mult),
                          reads=[("ps", 6), "small"], writes=["hT"])

                dump(P, "Ot", R1[:, 4120:6168], [("Ot", 0), ("Ot", 1)])
                dump(P, "mixT", hT[:], ["hT"])
                def proj_tok(off, nkg, src3, v, skey):
                    for ng in range(4):
                        for kg in range(nkg):
                            sl = wload(off + ng * nkg + kg)
                            for cc in range(4):
                                c = kg * 4 + cc
                                for s in range(2):
                                    first = (kg == 0 and cc == 0)
                                    last = (kg == nkg - 1 and cc == 3)
                                    bk = 2 * (ng % 2) + s
                                    E("pe", lambda e, sl=sl, cc=cc, c=c, s=s, first=first, last=last, bk=bk: e.matmul(
                                        ps[bk][:, :], lhsT=src3[:, c, s * 128:(s + 1) * 128], rhs=Wslot[sl][:, cc * 512:(cc + 1) * 512],
                                        start=first, stop=last),
                                      reads=[("W", sl), skey, "R1ph"] if cc in (0, 3) else (), writes=[("ps", bk)] if (first or last) else (),
                                      mark=(cc == 3))
                        for s in range(2):
                            bk = 2 * (ng % 2) + s
                            E("dve", lambda e, s=s, bk=bk, ng=ng: e.tensor_tensor(out=tE[:, s, :], in0=ps[bk][:, :], in1=gt3[:, v, ng * 512:(ng + 1) * 512],
                                                                                  op=ALU.mult), reads=[("ps", bk), ("gt", v)], writes=[("tE", s)])
                            E("pool", lambda e, s=s, ng=ng: e.tensor_tensor(out=xt3[:, s, ng * 512:(ng + 1) * 512], in0=xt3[:, s, ng * 512:(ng + 1) * 512],
                                                                            in1=tE[:, s, :], op=ALU.add), reads=[("tE", s), ("xt", s)], writes=[("xt", s)])

                proj_tok(OFF_OUT, 4, hT3, 0, "hT")
                dump(P, "x1", xt[:], [("xt", 0), ("xt", 1)])
                norm_to_hT(b, 1, ())
                dump(P, "h2T", hT[:], ["hT"])
                for j in range(NFF):
                    slg = wload(OFF_UP + 2 * j)
                    slu = wload(OFF_UP + 2 * j + 1)
                    bg = 4 * (j % 2)
                    bu = bg + 1
                    for (sl, bk) in ((slg, bg), (slu, bu)):
                        for c in range(16):
                            E("pe", lambda e, sl=sl, c=c, bk=bk: e.matmul(ps[bk][:, 0:T], lhsT=Wslot[sl][:, c * 128:(c + 1) * 128], rhs=hT3[:, c, :],
                                                                          start=(c == 0), stop=(c == 15)),
                              reads=[("W", sl), "hT"] if c in (0, 15) else (), writes=[("ps", bk)] if c in (0, 15) else (), mark=(c == 15))
                    t1 = tA[:, 10 + (j % 2), :]
                    sigmoid_from(t1, ps[bg][:, 0:T], t1, [("ps", bg)], "sg%d" % (j % 2))
                    E("dve", lambda e, t1=t1, bg=bg: e.tensor_tensor(out=t1, in0=t1, in1=ps[bg][:, 0:T], op=ALU.mult),
                      reads=["sg%d" % (j % 2), ("ps", bg)], writes=["sg%d" % (j % 2)])
                    first = ["R1ph"] if j == 0 else []
                    E("dve", lambda e, t1=t1, bu=bu, j=j: e.tensor_tensor(out=actT[:, j, :], in0=t1, in1=ps[bu][:, 0:T], op=ALU.mult),
                      reads=["sg%d" % (j % 2), ("ps", bu)] + ([] if j == 0 else ["R1ph"]), writes=["actT"] + first)
                dump(P, "actT", R1[:, 0:5632].bitcast(BF16), ["actT"])
                proj_tok(OFF_DN, 11, actT, 1, "actT")
                P.dma("pool", "xout", [(out_d[b, t0 + s * 128:t0 + (s + 1) * 128, :], xt3[:, s, :]) for s in range(2)],
                      reads=[("xt", 0), ("xt", 1)], writes=[("out", b, i)])
        fin = [v for k, v in P.last_w.items() if isinstance(k, tuple) and k[0] in ("out", "dbgout")]
        P.wait_all("pool", fin)
        P.run()
    P.close()
    return nc


_NC_CACHE = {}


def _stat_blocks(W):
    K, N = W.shape
    nch = N // 128
    return np.ascontiguousarray(W.reshape(16, 128, nch, 128).transpose(2, 1, 0, 3)).reshape(nch, 128, 2048)


def _mov_blocks(W):
    K, N = W.shape
    nkg = K // 512
    NG = N // 512
    return np.ascontiguousarray(W.reshape(nkg, 4, 128, NG, 512).transpose(3, 0, 2, 1, 4)).reshape(NG * nkg, 128, 2048)


def _fm(v):
    v = np.asarray(v, np.float32).reshape(-1, 128)
    return np.ascontiguousarray(v.T)


def kernel(x, c, w_ada, b_ada, g_mix, w_in, conv_w, conv_b, w_gate_a, b_gate_a, w_gate_x, b_gate_x,
           lru_logit, b_forget, g_q, g_k, g_out_rec, g_out_att, w_out, g_ffn, w_up, w_down):
    f = lambda a: np.asarray(a, np.float32)
    x = f(x); c = f(c)
    w_ada = f(w_ada)[0]; w_in = f(w_in)[0]; w_out = f(w_out)[0]; w_up = f(w_up)[0]; w_down = f(w_down)[0]
    gate_b = _stat_blocks(w_up[:, :DFF])
    up_b = _stat_blocks(w_up[:, DFF:])
    wsrc = np.concatenate([
        _stat_blocks(w_ada),
        _stat_blocks(w_in[:, 0:4096]),
        _mov_blocks(w_in[:, 4096:5120]),
        _mov_blocks(w_out),
        np.stack([gate_b, up_b], 1).reshape(2 * NFF, 128, 2048),
        _mov_blocks(w_down),
    ], 0)
    assert wsrc.shape[0] == N_ADA + N_CONV
    par = np.zeros((128, 160), np.float32)
    par[:, 0:16] = _fm(f(g_mix)[0])
    par[:, 16:32] = _fm(f(g_ffn)[0])
    par[:, 32:64] = f(conv_w)[0].reshape(4, 8, 128).transpose(2, 1, 0).reshape(128, 32)
    par[:, 64:72] = _fm(f(conv_b)[0])
    par[:, 72:80] = _fm(f(b_gate_a)[0])
    par[:, 80:88] = _fm(f(b_gate_x)[0])
    par[:, 88:96] = _fm(f(lru_logit)[0])
    par[:, 96:104] = _fm(f(g_out_rec)[0])
    par[:, 104:112] = _fm(f(g_out_att)[0])
    par[:, 112] = f(g_q)[0]
    par[:, 113] = f(g_k)[0]
    par[0:8, 114] = f(b_forget)[0]
    bada = _fm(f(b_ada)[0])
    wg = np.stack([f(w_gate_a)[0], f(w_gate_x)[0]], 0).transpose(2, 0, 1, 3).reshape(128, 2048)
    wf = w_in[:, 5120:5128].reshape(16, 128, 8).transpose(1, 0, 2).reshape(128, 128)
    con = np.concatenate([np.eye(128, dtype=np.float32), np.triu(np.ones((128, 128), np.float32)),
                          np.ones((128, 128), np.float32)], 1)
    n_cores = 8
    in_maps = []
    for ci in range(n_cores):
        c2 = c[2 * ci:2 * ci + 2]
        cT = c2.reshape(2, 16, 128).transpose(2, 1, 0).reshape(128, 32)
        small = np.ascontiguousarray(np.concatenate([par, bada, cT, wg, wf, con], 1).astype(np.float32))
        assert small.shape == (128, C_END)
        in_maps.append({"x": np.ascontiguousarray(x[2 * ci:2 * ci + 2]), "wsrc": wsrc, "small": small})
    if "nc" not in _NC_CACHE:
        _NC_CACHE["nc"] = build_nc()
    res = run_bass_kernel_spmd(_NC_CACHE["nc"], in_maps, core_ids=list(range(n_cores)))
    return np.concatenate([np.asarray(r["out"], np.float32) for r in res.results], 0)
```

```python
import numpy as np
import concourse.bass as bass
import concourse.mybir as mybir
from concourse.bass_utils import run_bass_kernel_spmd

F32 = mybir.dt.float32
BF16 = mybir.dt.bfloat16
AF = mybir.ActivationFunctionType
ALU = mybir.AluOpType

ENGS = ("pe", "act", "dve", "pool", "sp")

D = 2048
S = 2048
T = 256
NT = S // T
NSEQ = 2
DFF = 5632
NFF = DFF // 128
EPS = 1e-6
N_ADA = 96
N_CONV = 188
OFF_INS, OFF_INV, OFF_OUT, OFF_UP, OFF_DN = 0, 32, 40, 56, 144
C_PAR, C_BADA, C_CT, C_WG, C_WF, C_CON, C_END = 0, 160, 256, 288, 2336, 2464, 2848


class Prog:
    def __init__(self, nc):
        self.nc = nc
        self.q = {e: [] for e in ENGS}
        self.cnt = {e: 0 for e in ENGS}
        self.sems = {}
        self.seen = {e: {} for e in ENGS}
        self.last_w = {}
        self.readers = {}
        self.slot_cnt = {}
        self.ctx = []

    def sem(self, name):
        if name not in self.sems:
            cm = self.nc.semaphore(name)
            self.ctx.append(cm)
            self.sems[name] = cm.__enter__()
        return self.sems[name]

    def close(self):
        for cm in reversed(self.ctx):
            cm.__exit__(None, None, None)

    def _deps(self, eng, reads, writes, extra):
        deps = []
        for k in reads:
            t = self.last_w.get(k)
            if t is not None:
                deps.append(t)
        for k in writes:
            t = self.last_w.get(k)
            if t is not None and t[2] != eng:
                deps.append(t)
            for r in self.readers.get(k, ()):
                if r[2] != eng:
                    deps.append(r)
        deps.extend(extra)
        return deps

    def _commit(self, tok, reads, writes):
        for k in writes:
            self.last_w[k] = tok
            self.readers[k] = []
        for k in reads:
            self.readers.setdefault(k, []).append(tok)

    def _waits(self, eng, deps):
        best = {}
        for d in deps:
            if d is None:
                continue
            name, val, _ = d
            if self.seen[eng].get(name, 0) >= val:
                continue
            if best.get(name, 0) < val:
                best[name] = val
        for name, val in best.items():
            self.seen[eng][name] = val
        return list(best.items())

    def emit(self, eng, fn, reads=(), writes=(), extra=(), mark=True):
        deps = self._deps(eng, reads, writes, extra)
        waits = self._waits(eng, deps)
        tok = None
        if mark:
            self.cnt[eng] += 1
            tok = ("e_" + eng, self.cnt[eng], eng)
        self.q[eng].append((waits, fn, ("e_" + eng, 1) if mark else None))
        if mark:
            self._commit(tok, reads, writes)
        return tok

    def dma(self, eng, slot, pairs, reads=(), writes=(), extra=()):
        deps = self._deps("dma_none", reads, writes, extra)
        waits = self._waits(eng, deps)
        name = "d_" + slot
        self.sem(name)
        for i, (o, a) in enumerate(pairs):
            def fn(e, o=o, a=a):
                return e.dma_start(out=o, in_=a)
            self.q[eng].append((waits if i == 0 else [], fn, (name, 16)))
        self.slot_cnt[name] = self.slot_cnt.get(name, 0) + 16 * len(pairs)
        tok = (name, self.slot_cnt[name], "dma")
        self._commit(tok, reads, writes)
        return tok

    def wait_all(self, eng, toks):
        waits = self._waits(eng, toks)
        self.q[eng].append((waits, None, None))

    def run(self):
        nc = self.nc
        for e in ENGS:
            self.sem("e_" + e)
        sems = self.sems
        q = self.q

        def play(engine, lst):
            for waits, fn, inc in lst:
                for name, val in waits:
                    engine.wait_ge(sems[name], val)
                if fn is None:
                    continue
                ins = fn(engine)
                if inc is not None:
                    ins.then_inc(sems[inc[0]], inc[1])

        with nc.Block() as block:
            @block.sync
            def _(e):
                play(e, q["sp"])

            @block.tensor
            def _(e):
                play(e, q["pe"])

            @block.scalar
            def _(e):
                play(e, q["act"])

            @block.vector
            def _(e):
                play(e, q["dve"])

            @block.gpsimd
            def _(e):
                play(e, q["pool"])


DBG = None


def build_nc():
    nc = bass.Bass("TRN2", target_bir_lowering=False)
    dbg = DBG
    if dbg:
        d32 = nc.dram_tensor("d32", [128, 32768], F32, kind="ExternalOutput").ap()
        d16 = nc.dram_tensor("d16", [128, 65536], BF16, kind="ExternalOutput").ap()
    dpos = {"32": 0, "16": 0}
    dmap = {}

    def dump(P, name, ap, keys, np_=128):
        if not dbg:
            return
        is16 = ap.dtype == BF16
        kk = "16" if is16 else "32"
        n = 1
        for d_ in ap.shape[1:]:
            n *= d_
        o = dpos[kk]
        dpos[kk] += n
        dmap[name] = (kk, o, tuple(ap.shape))
        dst = (d16 if is16 else d32)[0:np_, o:o + n]
        if len(ap.shape) == 3:
            dst = dst.rearrange("p (a b) -> p a b", a=ap.shape[1])
        P.dma("pool", "dbg", [(dst, ap)], reads=keys, writes=[("dbgout", name)])
    build_nc.dmap = dmap

    x_d = nc.dram_tensor("x", [NSEQ, S, D], F32, kind="ExternalInput").ap()
    wsrc = nc.dram_tensor("wsrc", [N_ADA + N_CONV, 128, 2048], F32, kind="ExternalInput").ap()
    small_d = nc.dram_tensor("small", [128, C_END], F32, kind="ExternalInput").ap()
    out_d = nc.dram_tensor("out", [NSEQ, S, D], F32, kind="ExternalOutput").ap()
    wbf = nc.dram_tensor("wbf", [N_CONV, 128, 2048], BF16, kind="Internal").ap()

    P = Prog(nc)
    E = P.emit
    import contextlib
    with contextlib.ExitStack() as es:
        def sb(name, shape, dt):
            return es.enter_context(nc.sbuf_tensor("sb_" + name, shape, dt))

        def pst(name):
            return es.enter_context(nc.psum_tensor(name, [128, 512], F32))

        small = sb("small", [128, C_END], F32)
        der = sb("der", [128, 64], F32)
        cact = sb("cact", [128, 32], F32)
        ctmp = sb("ctmp", [128, 64], F32)
        modT = sb("modT", [128, 192], F32)
        ABt = sb("ABt", [128, 64], F32)
        xt = sb("xt", [128, 2 * D], F32)
        hT = sb("hT", [128, 16 * T], BF16)
        xn = sb("xn", [128, D], BF16)
        KT = sb("KT", [128, 8 * S], BF16)
        Vt = sb("Vt", [128, 16 * 8 * 130], BF16)
        R1 = sb("R1", [128, 7192], F32)
        tmpA = sb("tmpA", [128, 12 * T], F32)
        ubf = sb("ubf", [128, T], BF16)
        sqb = sb("sqb", [128, 2 * T], BF16)
        sqq = sb("sqq", [128, 2 * T], BF16)
        rrq = sb("rrq", [128, 2 * T], F32)
        rrt = sb("rrt", [128, 2 * T], F32)
        PT = sb("PT", [128, 3 * T], BF16)
        Wr = sb("Wr", [128, 4 * 2048], BF16)
        stg0 = sb("stg0", [128, 2048], F32)
        gtbc = sb("gtbc", [128, 2 * D], F32)
        tmpE = sb("tmpE", [128, 2 * 512], F32)
        Gt = sb("Gt", [128, 2 * 128], F32)
        wgb = sb("wgb", [128, 2048], BF16)
        wfb = sb("wfb", [128, 128], BF16)
        idb = sb("idb", [128, 128], BF16)
        trib = sb("trib", [128, 128], BF16)
        onesb = sb("onesb", [128, 128], BF16)
        cumT = sb("cumT", [128, 16 * 8], F32)
        biasT = sb("biasT", [128, 8 * 16], F32)
        stat = sb("stat", [128, 32], F32)
        cumr = sb("cumr", [8, 2 * T], F32)
        hcar = sb("hcar", [128, 8], F32)
        ccar = sb("ccar", [8, 2], F32)
        bcs = sb("bcs", [128, 16], F32)
        o256 = sb("o256", [8, T], F32)
        halo = sb("halo", [128, 24], F32)
        ps = [pst("ps%d" % i) for i in range(8)]

        par = small[:, C_PAR:C_BADA]
        bada = small[:, C_BADA:C_CT]
        cTt = small[:, C_CT:C_WG]
        wg32 = small[:, C_WG:C_WF]
        wf32 = small[:, C_WF:C_CON]
        ident_f = small[:, C_CON:C_CON + 128]
        tri_f = small[:, C_CON + 128:C_CON + 256]
        ones_f = small[:, C_CON + 256:C_CON + 384]

        stg = [stg0[:], Wr[:, 0:4096].bitcast(F32), Wr[:, 4096:8192].bitcast(F32), xt[:, 0:2048], xt[:, 2048:4096]]
        cbf = [gtbc[:, i * 1024:(i + 1) * 1024].bitcast(BF16) for i in range(4)]
        Wslot = [Wr[:, i * 2048:(i + 1) * 2048] for i in range(4)]
        xr = R1[:, 0:2072].rearrange("p (j t) -> p j t", j=8)
        gy = R1[:, 2072:4120].rearrange("p (j t) -> p j t", j=8)
        Ot = R1[:, 4120:6168].rearrange("p (s f) -> p s f", s=2)
        qT = R1[:, 6168:7192].bitcast(BF16).rearrange("p (h t) -> p h t", h=8)
        actT = R1[:, 0:5632].bitcast(BF16).rearrange("p (j t) -> p j t", j=NFF)
        hT3 = hT[:].rearrange("p (c t) -> p c t", c=16)
        xt3 = xt[:].rearrange("p (s f) -> p s f", s=2)
        KT3 = KT[:].rearrange("p (h t) -> p h t", h=8)
        Vt4 = Vt[:].rearrange("p (b h d) -> p b h d", b=16, h=8)
        tA = tmpA[:].rearrange("p (k t) -> p k t", k=12)
        halo3 = halo[:].rearrange("p (j k) -> p j k", j=8)
        PT3 = PT[:].rearrange("p (k t) -> p k t", k=3)
        gt3 = gtbc[:].rearrange("p (v f) -> p v f", v=2)
        tE = tmpE[:].rearrange("p (k f) -> p k f", k=2)
        cumT3 = cumT[:].rearrange("p (b h) -> p b h", b=16)
        biasT3 = biasT[:].rearrange("p (h b) -> p h b", h=8)
        modT3 = modT[:].rearrange("p (c b) -> p c b", b=2)
        AB4 = ABt[:].rearrange("p (b w c) -> p b w c", b=2, w=2)
        psb = [p_[:].bitcast(BF16) for p_ in ps]

        P.dma("sp", "small", [(small[:], small_d[:, :])], writes=["small"])
        E("dve", lambda e: e.tensor_copy(out=idb[:], in_=ident_f), reads=["small"], writes=["idb"])
        E("dve", lambda e: e.tensor_copy(out=trib[:], in_=tri_f), reads=["small"], writes=["trib"])
        E("dve", lambda e: e.tensor_copy(out=onesb[:], in_=ones_f), reads=["small"], writes=["onesb"])
        E("dve", lambda e: e.tensor_copy(out=wgb[:], in_=wg32), reads=["small"], writes=["wgb"])
        E("dve", lambda e: e.tensor_copy(out=wfb[:], in_=wf32), reads=["small"], writes=["wfb"])
        E("pool", lambda e: e.memset(Vt[:], 1.0), writes=["Vt"])
        E("pool", lambda e: e.memset(o256[:], 1.0), writes=["ones256"])
        E("act", lambda e: e.activation(out=ctmp[:, 0:8], in_=par[:, 88:96], func=AF.Exp, scale=-1.0),
          reads=["small"], writes=["ctmp"])
        E("act", lambda e: e.activation(out=ctmp[:, 8:16], in_=ctmp[:, 0:8], func=AF.Ln, bias=1.0),
          reads=["ctmp"], writes=["ctmp2"])
        E("dve", lambda e: e.tensor_scalar(out=der[:, 0:8], in0=ctmp[:, 8:16], scalar1=-8.0, scalar2=None, op0=ALU.mult),
          reads=["ctmp2"], writes=["der"])
        E("dve", lambda e: e.tensor_scalar(out=der[:, 8:16], in0=ctmp[:, 8:16], scalar1=-16.0, scalar2=None, op0=ALU.mult),
          reads=["ctmp2"], writes=["der"])
        E("dve", lambda e: e.tensor_scalar(out=der[:, 16:32], in0=par[:, 72:88], scalar1=-1.0, scalar2=None, op0=ALU.mult),
          reads=["small"], writes=["der"])
        E("dve", lambda e: e.tensor_scalar(out=der[:, 32:33], in0=par[:, 112:113], scalar1=128.0 ** -0.5, scalar2=None, op0=ALU.mult),
          reads=["small"], writes=["der"])
        E("dve", lambda e: e.tensor_scalar(out=der[:, 33:34], in0=par[:, 114:115], scalar1=-1.0, scalar2=None, op0=ALU.mult),
          reads=["small"], writes=["der"])
        E("act", lambda e: e.activation(out=ctmp[:, 16:48], in_=cTt, func=AF.Exp, scale=-1.0), reads=["small"], writes=["ctmp3"])
        E("act", lambda e: e.activation(out=ctmp[:, 16:48], in_=ctmp[:, 16:48], func=AF.Ln, bias=1.0), reads=["ctmp3"], writes=["ctmp3"])
        E("act", lambda e: e.activation(out=ctmp[:, 16:48], in_=ctmp[:, 16:48], func=AF.Exp, scale=-1.0), reads=["ctmp3"], writes=["ctmp3"])
        E("dve", lambda e: e.tensor_tensor(out=cact[:], in0=ctmp[:, 16:48], in1=cTt, op=ALU.mult), reads=["ctmp3", "small"], writes=["cact"])

        def SK(sl):
            if sl in (1, 2):
                return [("stg", sl), ("W", 2 * sl - 2), ("W", 2 * sl - 1)]
            if sl in (3, 4):
                return [("stg", sl), ("xt", sl - 3)]
            return [("stg", sl)]

        def CK(cs):
            return [("cbf", cs), ("gt", 0), ("gt", 1)]

        order = []
        ia = 0
        for k in range(N_CONV):
            order.append(("c", k))
            if k % 2 == 1 and ia < N_ADA:
                order.append(("a", ia))
                ia += 1
        while ia < N_ADA:
            order.append(("a", ia))
            ia += 1
        pend = []
        for n, (kind, k) in enumerate(order):
            sl = n % 5
            if kind == "a":
                P.dma("sp", "stg%d" % sl, [(stg[sl], wsrc[k, :, :])], writes=SK(sl))
                for c in range(16):
                    E("pe", lambda e, sl=sl, c=c, nb=k: e.matmul(ps[0][:, 2 * nb:2 * nb + 2], lhsT=stg[sl][:, c * 128:(c + 1) * 128],
                                                                  rhs=cact[:, 2 * c:2 * c + 2], start=(c == 0), stop=(c == 15)),
                      reads=SK(sl) + ["cact"] if c in (0, 15) else (), writes=[("ps", 0)] if c == 15 else (), mark=(c == 15))
            else:
                cs = k % 4
                P.dma("sp", "stg%d" % sl, [(stg[sl], wsrc[N_ADA + k, :, :])], writes=SK(sl))
                E("act", lambda e, sl=sl, cs=cs: e.activation(out=cbf[cs], in_=stg[sl], func=AF.Copy), reads=SK(sl), writes=CK(cs))
                P.dma("act", "cbf%d" % cs, [(wbf[k, :, :], cbf[cs])], reads=CK(cs), writes=[("wbf", k)])
        for b in range(2):
            E("dve", lambda e, b=b: e.tensor_tensor(out=modT3[:, :, b], in0=ps[0][:, 0:192].rearrange("p (c b) -> p c b", b=2)[:, :, b],
                                                    in1=bada, op=ALU.add), reads=[("ps", 0), "small"], writes=["modT"])
            E("dve", lambda e, b=b: e.scalar_tensor_tensor(out=AB4[:, b, 0, :], in0=modT3[:, 16:32, b], scalar=1.0, in1=par[:, 0:16],
                                                           op0=ALU.add, op1=ALU.mult), reads=["modT"], writes=["AB"])
            E("dve", lambda e, b=b: e.scalar_tensor_tensor(out=AB4[:, b, 1, :], in0=modT3[:, 64:80, b], scalar=1.0, in1=par[:, 16:32],
                                                           op0=ALU.add, op1=ALU.mult), reads=["modT"], writes=["AB"])

        wctr = [0]

        def wload(k):
            sl = wctr[0] % 4
            wctr[0] += 1
            P.dma("sp", "W%d" % sl, [(Wslot[sl], wbf[k, :, :])], reads=[("wbf", k)], writes=[("W", sl)])
            return sl

        psrot = [0]

        def nextbank(lo, n):
            b = lo + psrot[0] % n
            psrot[0] += 1
            return b

        def sigmoid_from(out_ap, in_ap, t1, rkeys, wkey, neg_bias=None, scale=1.0):
            kw = {} if neg_bias is None else {"bias": neg_bias}
            E("act", lambda e: e.activation(out=t1, in_=in_ap, func=AF.Exp, scale=-scale, **kw), reads=rkeys, writes=[wkey + "_t", wkey])
            E("act", lambda e: e.activation(out=t1, in_=t1, func=AF.Ln, bias=1.0), reads=[wkey + "_t"], writes=[wkey + "_t"])
            return E("act", lambda e: e.activation(out=out_ap, in_=t1, func=AF.Exp, scale=-1.0), reads=[wkey + "_t"], writes=[wkey])

        def norm_to_hT(b, which, first_extra_keys):
            Bcol0 = 0 if which == 0 else 48
            for s in range(2):
                E("act", lambda e, s=s: e.activation(out=stg0[:], in_=xt3[:, s, :], func=AF.Square, accum_out=stat[:, s:s + 1]),
                  reads=[("xt", s)], writes=[("stg", 0), ("stat", s)])
                E("act", lambda e, s=s: e.activation(out=stat[:, 2 + s:3 + s], in_=stat[:, s:s + 1], func=AF.Ln, scale=1.0 / D, bias=EPS),
                  reads=[("stat", s)], writes=[("stat2", s)])
                E("act", lambda e, s=s: e.activation(out=stat[:, 4 + s:5 + s], in_=stat[:, 2 + s:3 + s], func=AF.Exp, scale=-0.5),
                  reads=[("stat2", s)], writes=[("stat3", s)])
                E("act", lambda e, s=s: e.activation(out=xn[:], in_=xt3[:, s, :], func=AF.Copy, scale=stat[:, 4 + s:5 + s]),
                  reads=[("xt", s), ("stat3", s)], writes=["xn"])
                for g in range(2):
                    bk = 6 + g
                    for c8 in range(8):
                        c = g * 8 + c8
                        E("pe", lambda e, c=c, c8=c8, bk=bk: e.transpose(out=psb[bk][:, c8 * 128:(c8 + 1) * 128],
                                                                         in_=xn[:, c * 128:(c + 1) * 128], identity=idb[:]),
                          reads=["xn", "idb"] if c8 in (0, 7) else (), writes=[("ps", bk)] if c8 in (0, 7) else (), mark=(c8 == 7))
                    for c8 in range(8):
                        c = g * 8 + c8
                        E("dve", lambda e, c=c, c8=c8, bk=bk, s=s: e.tensor_scalar(
                            out=hT3[:, c, s * 128:(s + 1) * 128], in0=psb[bk][:, c8 * 128:(c8 + 1) * 128],
                            scalar1=AB4[:, b, which, c:c + 1], scalar2=modT3[:, Bcol0 + c, b:b + 1], op0=ALU.mult, op1=ALU.add),
                          reads=[("ps", bk), "AB", "modT"], writes=["hT"])

        dump(P, "modT", modT[:], ["modT"])
        dump(P, "AB", ABt[:], ["AB"])
        dump(P, "der", der[:], ["der"])
        for b in range(1 if dbg else NSEQ):
            for v in range(2):
                ch0 = 32 if v == 0 else 80
                for cc in range(16):
                    gsl = cc % 2
                    E("dve", lambda e, gsl=gsl, ch=ch0 + cc, b=b: e.tensor_scalar(out=Gt[:, gsl * 128:(gsl + 1) * 128], in0=ones_f,
                                                                           scalar1=modT3[:, ch, b:b + 1], scalar2=None, op0=ALU.mult),
                      reads=["modT", "small"], writes=[("Gt", gsl)])
                    E("pe", lambda e, gsl=gsl, cc=cc: e.matmul(ps[5][:, (cc % 4) * 128:(cc % 4 + 1) * 128], lhsT=Gt[:, gsl * 128:(gsl + 1) * 128],
                                                               rhs=ident_f, start=True, stop=True),
                      reads=[("Gt", gsl), "small"], writes=[("ps", 5)])
                    if cc % 4 == 3:
                        q4 = cc // 4
                        E("act", lambda e, v=v, q4=q4: e.activation(out=gt3[:, v, q4 * 512:(q4 + 1) * 512], in_=ps[5][:], func=AF.Copy),
                          reads=[("ps", 5)], writes=[("gt", v)])
            dump(P, "gt", gtbc[:], [("gt", 0), ("gt", 1)])
            for i in range(1 if dbg else NT):
                t0 = i * T
                P.dma("pool", "xin", [(xt3[:, s, :], x_d[b, t0 + s * 128:t0 + (s + 1) * 128, :]) for s in range(2)],
                      writes=[("xt", 0), ("xt", 1)])
                norm_to_hT(b, 0, ())
                dump(P, "hT1", hT[:], ["hT"])
                qk_pending = [None]
                for j in range(32):
                    sl = wload(OFF_INS + j)
                    bk = nextbank(0, 4)
                    for c in range(16):
                        E("pe", lambda e, sl=sl, c=c, bk=bk: e.matmul(ps[bk][:, 0:T], lhsT=Wslot[sl][:, c * 128:(c + 1) * 128],
                                                                      rhs=hT3[:, c, :], start=(c == 0), stop=(c == 15)),
                          reads=[("W", sl), "hT"] if c in (0, 15) else (), writes=[("ps", bk)] if c in (0, 15) else (), mark=(c == 15))
                    pj = ps[bk][:, 0:T]
                    if j < 8:
                        first = ["R1ph"] if j == 0 else []
                        E("act", lambda e, j=j, pj=pj: e.activation(out=xr[:, j, 3:3 + T], in_=pj, func=AF.Copy),
                          reads=[("ps", bk)] + ([] if j == 0 else ["R1ph"]), writes=[("xr", j)] + first)
                        if i == 0:
                            E("pool", lambda e, j=j: e.memset(xr[:, j, 0:3], 0.0), reads=[("xr", j)], writes=[("xrh", j)])
                        else:
                            E("pool", lambda e, j=j: e.tensor_copy(out=xr[:, j, 0:3], in_=halo3[:, j, :]), reads=[("xr", j), ("halo", j)], writes=[("xrh", j)])
                    elif j < 16:
                        jj = j - 8
                        E("act", lambda e, pj=pj: e.activation(out=tA[:, 8, :], in_=pj, func=AF.Square), reads=[("ps", bk), "R1ph"], writes=["g1"])
                        E("dve", lambda e, pj=pj: e.scalar_tensor_tensor(out=tA[:, 8, :], in0=tA[:, 8, :], scalar=1.0 / 0.044715, in1=pj,
                                                                         op0=ALU.add, op1=ALU.mult), reads=["g1", ("ps", bk)], writes=["g2"])
                        sigmoid_from(tA[:, 9, :], tA[:, 8, :], tA[:, 9, :], ["g2"], "g3", scale=1.5957691216 * 0.044715)
                        E("dve", lambda e, jj=jj, pj=pj: e.tensor_tensor(out=gy[:, jj, :], in0=tA[:, 9, :], in1=pj, op=ALU.mult),
                          reads=["g3", ("ps", bk), "R1ph"], writes=[("gy", jj)])
                    else:
                        h = (j - 16) % 8
                        isq = j < 24
                        par_ = j % 2
                        E("act", lambda e, pj=pj, par_=par_: e.activation(out=sqq[:, par_ * T:(par_ + 1) * T], in_=pj, func=AF.Square),
                          reads=[("ps", bk)], writes=[("sqq", par_)])

                        def finish(h=h, isq=isq, par_=par_, pj=pj, bk=bk, t0=t0):
                            nb_ = 4 + par_
                            E("pe", lambda e: e.matmul(ps[nb_][:, 0:T], lhsT=onesb[:], rhs=sqq[:, par_ * T:(par_ + 1) * T], start=True, stop=True),
                              reads=[("sqq", par_), "onesb"], writes=[("ps", nb_)])
                            E("act", lambda e: e.activation(out=rrq[:, par_ * T:(par_ + 1) * T], in_=ps[nb_][:, 0:T], func=AF.Ln, scale=1.0 / 128, bias=EPS),
                              reads=[("ps", nb_)], writes=[("rrq", par_)])
                            E("act", lambda e: e.activation(out=rrq[:, par_ * T:(par_ + 1) * T], in_=rrq[:, par_ * T:(par_ + 1) * T], func=AF.Exp, scale=-0.5),
                              reads=[("rrq", par_)], writes=[("rrq", par_)])
                            if isq:
                                E("dve", lambda e: e.scalar_tensor_tensor(out=qT[:, h, :], in0=pj, scalar=der[:, 32:33], in1=rrq[:, par_ * T:(par_ + 1) * T],
                                                                          op0=ALU.mult, op1=ALU.mult),
                                  reads=[("ps", bk), ("rrq", par_), "der", "R1ph"], writes=[("qT", h)])
                            else:
                                E("dve", lambda e: e.scalar_tensor_tensor(out=KT3[:, h, t0:t0 + T], in0=pj, scalar=par[:, 113:114], in1=rrq[:, par_ * T:(par_ + 1) * T],
                                                                          op0=ALU.mult, op1=ALU.mult),
                                  reads=[("ps", bk), ("rrq", par_), "small"], writes=[("KT", h)])
                        if qk_pending[0] is not None:
                            qk_pending[0]()
                        qk_pending[0] = finish
                if qk_pending[0] is not None:
                    qk_pending[0]()
                    qk_pending[0] = None
                for ng in range(2):
                    for kg in range(4):
                        sl = wload(OFF_INV + ng * 4 + kg)
                        for cc in range(4):
                            c = kg * 4 + cc
                            for s in range(2):
                                first = (kg == 0 and cc == 0)
                                last = (kg == 3 and cc == 3)
                                E("pe", lambda e, sl=sl, cc=cc, c=c, s=s, first=first, last=last: e.matmul(
                                    ps[s][:, :], lhsT=hT3[:, c, s * 128:(s + 1) * 128], rhs=Wslot[sl][:, cc * 512:(cc + 1) * 512],
                                    start=first, stop=last),
                                  reads=[("W", sl), "hT"] if cc in (0, 3) else (), writes=[("ps", s)] if (first or last) else (),
                                  mark=(cc == 3))
                    for s in range(2):
                        blk = t0 // 128 + s
                        E("act", lambda e, s=s, blk=blk, ng=ng: e.activation(
                            out=Vt4[:, blk, ng * 4:(ng + 1) * 4, 0:128], in_=ps[s][:, :].rearrange("p (h d) -> p h d", h=4), func=AF.Copy),
                          reads=[("ps", s)], writes=["Vt"])
                for c in range(16):
                    E("pe", lambda e, c=c: e.matmul(ps[2][0:8, 0:T], lhsT=wfb[:, c * 8:(c + 1) * 8], rhs=hT3[:, c, :], start=(c == 0), stop=(c == 15)),
                      reads=["wfb", "hT"] if c in (0, 15) else (), writes=[("ps", 2)] if c in (0, 15) else (), mark=(c == 15))
                E("act", lambda e: e.activation(out=cumr[:, 0:T], in_=ps[2][0:8, 0:T], func=AF.Exp, scale=-1.0, bias=der[0:8, 33:34]),
                  reads=[("ps", 2), "der"], writes=["cumr0"])
                E("act", lambda e: e.activation(out=cumr[:, 0:T], in_=cumr[:, 0:T], func=AF.Ln, bias=1.0), reads=["cumr0"], writes=["cumr0"])
                if i == 0:
                    E("dve", lambda e: e.tensor_tensor_scan(out=cumr[:, T:2 * T], data0=o256[:], data1=cumr[:, 0:T], initial=0.0,
                                                            op0=ALU.mult, op1=ALU.subtract),
                      reads=["cumr0", "ones256"], writes=["cumr1"])
                else:
                    E("dve", lambda e: e.tensor_tensor_scan(out=cumr[:, T:2 * T], data0=o256[:], data1=cumr[:, 0:T], initial=ccar[:, 0:1],
                                                            op0=ALU.mult, op1=ALU.subtract),
                      reads=["cumr0", "ccar", "ones256"], writes=["cumr1"])
                E("dve", lambda e: e.tensor_copy(out=ccar[:, 0:1], in_=cumr[:, 2 * T - 1:2 * T]), reads=["cumr1"], writes=["ccar"])
                for s in range(2):
                    blk = t0 // 128 + s
                    E("pe", lambda e, s=s: e.transpose(out=ps[3][:, s * 8:(s + 1) * 8], in_=cumr[:, T + s * 128:T + (s + 1) * 128], identity=ident_f[0:8, 0:8]),
                      reads=["cumr1", "small"], writes=[("ps", 3)])
                    E("dve", lambda e, s=s, blk=blk: e.tensor_copy(out=cumT3[:, blk, :], in_=ps[3][:, s * 8:(s + 1) * 8]),
                      reads=[("ps", 3)], writes=["cumT"])
                E("dve", lambda e: e.tensor_scalar(out=ctmp[0:8, 48:56], in0=ident_f[0:8, 0:8], scalar1=cumr[:, T + 127:T + 128], scalar2=None, op0=ALU.mult),
                  reads=["cumr1", "small"], writes=["dg"])
                E("pe", lambda e: e.matmul(ps[3][:, 16:24], lhsT=ones_f[0:8, :], rhs=ctmp[0:8, 48:56], start=True, stop=True),
                  reads=["dg", "small"], writes=[("ps", 3)])
                E("dve", lambda e: e.tensor_copy(out=bcs[:, 0:8], in_=ps[3][:, 16:24]), reads=[("ps", 3)], writes=["bcs"])
                nkb = t0 // 128 + 2
                for h in range(8):
                    E("dve", lambda e, h=h, nkb=nkb: e.tensor_scalar(out=biasT3[:, h, 0:nkb], in0=cumT3[:, 0:nkb, h], scalar1=-1.0, scalar2=bcs[:, h:h + 1],
                                                            op0=ALU.mult, op1=ALU.add), reads=["cumT", "bcs"], writes=["biasT"])

                dump(P, "xr", R1[:, 0:2072], [("xr", j) for j in range(8)] + [("xrh", j) for j in range(8)])
                dump(P, "gyB", R1[:, 2072:4120], [("gy", j) for j in range(8)])
                dump(P, "qT", R1[:, 6168:7192].bitcast(BF16), [("qT", h) for h in range(8)])
                dump(P, "KT", KT3[:, :, 0:T], [("KT", h) for h in range(8)])
                dump(P, "Vt", Vt[:, 0:2 * 8 * 130], ["Vt"])
                dump(P, "cumr", cumr[:], ["cumr1"], np_=8)
                dump(P, "cumT", cumT[:], ["cumT"])
                dump(P, "biasT", biasT[:], ["biasT"])
                deferred = []

                def rg_part1(j):
                    cw = lambda k: par[:, 32 + j * 4 + k:33 + j * 4 + k]
                    E("dve", lambda e: e.tensor_scalar(out=tA[:, 0, :], in0=xr[:, j, 3:3 + T], scalar1=cw(3), scalar2=par[:, 64 + j:65 + j],
                                                       op0=ALU.mult, op1=ALU.add), reads=[("xr", j), ("xrh", j), "small"], writes=["u"])
                    for k in range(3):
                        E("dve", lambda e, k=k: e.scalar_tensor_tensor(out=tA[:, 0, :], in0=xr[:, j, k:k + T], scalar=cw(k), in1=tA[:, 0, :],
                                                                       op0=ALU.mult, op1=ALU.add), reads=[("xr", j), ("xrh", j), "u"], writes=["u"])
                    E("pool", lambda e: e.tensor_copy(out=ubf[:], in_=tA[:, 0, :]), reads=["u"], writes=["ubf"])
                    E("pool", lambda e: e.tensor_copy(out=halo3[:, j, :], in_=xr[:, j, T:T + 3]), reads=[("xr", j), "R1ph"], writes=[("halo", j)])
                    E("pe", lambda e: e.matmul(ps[2][:, 0:T], lhsT=wgb[:, j * 128:(j + 1) * 128], rhs=ubf[:], start=True, stop=True),
                      reads=["ubf", "wgb"], writes=[("ps", 2)])
                    E("pe", lambda e: e.matmul(ps[2][:, T:2 * T], lhsT=wgb[:, 1024 + j * 128:1024 + (j + 1) * 128], rhs=ubf[:], start=True, stop=True),
                      reads=["ubf", "wgb"], writes=[("ps", 2)])

                def rg_part2a(j):
                    sigmoid_from(tA[:, 1, :], ps[2][:, 0:T], tA[:, 1, :], [("ps", 2), "der"], "r", neg_bias=der[:, 16 + j:17 + j])
                    sigmoid_from(tA[:, 2, :], ps[2][:, T:2 * T], tA[:, 2, :], [("ps", 2), "der"], "ig", neg_bias=der[:, 24 + j:25 + j])
                    E("act", lambda e: e.activation(out=tA[:, 3, :], in_=tA[:, 1, :], func=AF.Exp, scale=der[:, j:j + 1]), reads=["r", "der"], writes=["a"])
                    E("act", lambda e: e.activation(out=tA[:, 4, :], in_=tA[:, 1, :], func=AF.Exp, scale=der[:, 8 + j:9 + j]), reads=["r", "der"], writes=["m"])
                    E("act", lambda e: e.activation(out=tA[:, 4, :], in_=tA[:, 4, :], func=AF.Ln, scale=-1.0, bias=1.0), reads=["m"], writes=["m"])
                    E("act", lambda e: e.activation(out=tA[:, 4, :], in_=tA[:, 4, :], func=AF.Exp, scale=0.5), reads=["m"], writes=["m"])

                def rg_part2b(j, i=i):
                    E("pool", lambda e: e.tensor_tensor(out=tA[:, 5, :], in0=tA[:, 4, :], in1=tA[:, 2, :], op=ALU.mult), reads=["m", "ig"], writes=["bb"])
                    E("pool", lambda e: e.tensor_tensor(out=tA[:, 5, :], in0=tA[:, 5, :], in1=tA[:, 0, :], op=ALU.mult), reads=["bb", "u"], writes=["bb"])
                    if i == 0:
                        E("dve", lambda e: e.tensor_tensor_scan(out=tA[:, 6, :], data0=tA[:, 3, :], data1=tA[:, 5, :], initial=0.0,
                                                                op0=ALU.mult, op1=ALU.add), reads=["a", "bb"], writes=["hh"])
                    else:
                        E("dve", lambda e: e.tensor_tensor_scan(out=tA[:, 6, :], data0=tA[:, 3, :], data1=tA[:, 5, :], initial=hcar[:, j:j + 1],
                                                                op0=ALU.mult, op1=ALU.add), reads=["a", "bb", "hcar"], writes=["hh"])
                    E("dve", lambda e: e.tensor_copy(out=hcar[:, j:j + 1], in_=tA[:, 6, T - 1:T]), reads=["hh"], writes=["hcar"])
                    E("dve", lambda e: e.tensor_tensor(out=gy[:, j, :], in0=tA[:, 6, :], in1=gy[:, j, :], op=ALU.mult),
                      reads=["hh", ("gy", j)], writes=[("gy", j)])
                    pq = j % 2
                    E("act", lambda e: e.activation(out=sqb[:, pq * T:(pq + 1) * T], in_=gy[:, j, :], func=AF.Square), reads=[("gy", j)], writes=[("sqb2", pq)])
                    deferred.append(lambda: E("pe", lambda e: e.matmul(ps[3][:, 0:T], lhsT=onesb[:], rhs=sqb[:, pq * T:(pq + 1) * T], start=(j == 0), stop=(j == 7)),
                                              reads=[("sqb2", pq), "onesb"], writes=[("ps", 3)]))

                steps = [(h, kb) for h in range(8) for kb in range(nkb)]

                def at_qk(n):
                    h, kb = steps[n]
                    c0 = 128 if kb == nkb - 1 else 0
                    sb_ = n % 2
                    E("pe", lambda e: e.matmul(ps[sb_][:, c0:T], lhsT=KT3[:, h, kb * 128:(kb + 1) * 128], rhs=qT[:, h, c0:T], start=True, stop=True),
                      reads=[("KT", h), ("qT", h)], writes=[("ps", sb_)])

                def at_exp(n):
                    h, kb = steps[n]
                    c0 = 128 if kb == nkb - 1 else 0
                    sb_ = n % 2
                    pslot = n % 3
                    E("act", lambda e: e.activation(out=PT3[:, pslot, c0:T], in_=ps[sb_][:, c0:T], func=AF.Exp, bias=biasT3[:, h, kb:kb + 1]),
                      reads=[("ps", sb_), "biasT"], writes=[("PT", pslot)])
                    if kb >= nkb - 2:
                        dc = 0 if kb == nkb - 2 else 128
                        E("pool", lambda e: e.tensor_tensor(out=PT3[:, pslot, dc:dc + 128], in0=PT3[:, pslot, dc:dc + 128], in1=trib[:], op=ALU.mult),
                          reads=[("PT", pslot), "trib"], writes=[("PT", pslot)])

                def at_pv(n):
                    h, kb = steps[n]
                    c0 = 128 if kb == nkb - 1 else 0
                    pslot = n % 3
                    ob = 4 + 2 * (h % 2)
                    for s in range(2):
                        if s * 128 < c0:
                            continue
                        lastkb = nkb - 2 + s
                        E("pe", lambda e, s=s, lastkb=lastkb: e.matmul(ps[ob + s][:, 0:129], lhsT=PT3[:, pslot, s * 128:(s + 1) * 128], rhs=Vt4[:, kb, h, 0:129],
                                                                       start=(kb == 0), stop=(kb == lastkb)),
                          reads=[("PT", pslot), "Vt"], writes=[("ps", ob + s)])

                def at_evac(h):
                    ob = 4 + 2 * (h % 2)
                    for s in range(2):
                        col = 8 + 2 * (h % 2) + s
                        E("dve", lambda e, s=s, col=col: e.reciprocal(out=stat[:, col:col + 1], in_=ps[ob + s][:, 128:129]),
                          reads=[("ps", ob + s)], writes=[("rl", h % 2, s)])
                        E("dve", lambda e, s=s, col=col: e.tensor_scalar(out=Ot[:, s, h * 128:(h + 1) * 128], in0=ps[ob + s][:, 0:128],
                                                                        scalar1=stat[:, col:col + 1], scalar2=None, op0=ALU.mult),
                          reads=[("ps", ob + s), ("rl", h % 2, s), "R1ph"], writes=[("Ot", s)])

                rg_part1(0)
                at_qk(0)
                for n in range(len(steps)):
                    h, kb = steps[n]
                    if kb == 0:
                        rg_part2a(h)
                    if n + 1 < len(steps):
                        at_qk(n + 1)
                    at_exp(n)
                    at_pv(n)
                    if kb == nkb - 1:
                        rg_part2b(h)
                        if h + 1 < 8:
                            rg_part1(h + 1)
                        at_evac(h)
                        while len(deferred) > 1:
                            deferred.pop(0)()
                while deferred:
                    deferred.pop(0)()
                E("act", lambda e: e.activation(out=rrt[:, T:2 * T], in_=ps[3][:, 0:T], func=AF.Ln, scale=1.0 / 1024, bias=EPS), reads=[("ps", 3)], writes=["rr2"])
                E("act", lambda e: e.activation(out=rrt[:, T:2 * T], in_=rrt[:, T:2 * T], func=AF.Exp, scale=-0.5), reads=["rr2"], writes=["rr2"])
                for j in range(8):
                    E("dve", lambda e, j=j: e.scalar_tensor_tensor(out=hT3[:, j, :], in0=gy[:, j, :], scalar=par[:, 96 + j:97 + j], in1=rrt[:, T:2 * T],
                                                                   op0=ALU.mult, op1=ALU.mult), reads=[("gy", j), "rr2", "small"], writes=["hT"])
                dump(P, "yrec", R1[:, 2072:4120], [("gy", j) for j in range(8)])
                dump(P, "hTrec", hT[:, 0:8 * T], ["hT"])
                for s in range(2):
                    E("act", lambda e, s=s: e.activation(out=stg0[:, 0:1024], in_=Ot[:, s, :], func=AF.Square, accum_out=stat[:, 12 + s:13 + s]),
                      reads=[("Ot", s)], writes=[("stg", 0), ("st4", s)])
                    E("act", lambda e, s=s: e.activation(out=stat[:, 14 + s:15 + s], in_=stat[:, 12 + s:13 + s], func=AF.Ln, scale=1.0 / 1024, bias=EPS),
                      reads=[("st4", s)], writes=[("st5", s)])
                    E("act", lambda e, s=s: e.activation(out=stat[:, 16 + s:17 + s], in_=stat[:, 14 + s:15 + s], func=AF.Exp, scale=-0.5),
                      reads=[("st5", s)], writes=[("st6", s)])
                    E("act", lambda e, s=s: e.activation(out=xn[:, 0:1024], in_=Ot[:, s, :], func=AF.Copy, scale=stat[:, 16 + s:17 + s]),
                      reads=[("Ot", s), ("st6", s)], writes=["xn"])
                    for h in range(8):
                        E("pe", lambda e, h=h: e.transpose(out=psb[6][:, h * 128:(h + 1) * 128], in_=xn[:, h * 128:(h + 1) * 128], identity=idb[:]),
                          reads=["xn", "idb"] if h in (0, 7) else (), writes=[("ps", 6)] if h in (0, 7) else (), mark=(h == 7))
                    for h in range(8):
                        E("dve", lambda e, h=h, s=s: e.tensor_scalar(out=hT3[:, 8 + h, s * 128:(s + 1) * 128], in0=psb[6][:, h * 128:(h + 1) * 128],
                                                                    scalar1=par[:, 104 + h:105 + h], scalar2=None, op0=ALU.mult),
                          reads=[("ps", 6), "small"], writes=["hT"])

                dump(P, "Ot", R1[:, 4120:6168], [("Ot", 0), ("Ot", 1)])
                dump(P, "mixT", hT[:], ["hT"])
                def proj_tok(off, nkg, src3, v, skey):
                    for ng in range(4):
                        for kg in range(nkg):
                            sl = wload(off + ng * nkg + kg)
                            for cc in range(4):
                                c = kg * 4 + cc
                                for s in range(2):
                                    first = (kg == 0 and cc == 0)
                                    last = (kg == nkg - 1 and cc == 3)
                                    bk = 2 * (ng % 2) + s
                                    E("pe", lambda e, sl=sl, cc=cc, c=c, s=s, first=first, last=last, bk=bk: e.matmul(
                                        ps[bk][:, :], lhsT=src3[:, c, s * 128:(s + 1) * 128], rhs=Wslot[sl][:, cc * 512:(cc + 1) * 512],
                                        start=first, stop=last),
                                      reads=[("W", sl), skey, "R1ph"] if cc in (0, 3) else (), writes=[("ps", bk)] if (first or last) else (),
                                      mark=(cc == 3))
                        for s in range(2):
                            bk = 2 * (ng % 2) + s
                            E("dve", lambda e, s=s, bk=bk, ng=ng: e.tensor_tensor(out=tE[:, s, :], in0=ps[bk][:, :], in1=gt3[:, v, ng * 512:(ng + 1) * 512],
                                                                                  op=ALU.mult), reads=[("ps", bk), ("gt", v)], writes=[("tE", s)])
                            E("pool", lambda e, s=s, ng=ng: e.tensor_tensor(out=xt3[:, s, ng * 512:(ng + 1) * 512], in0=xt3[:, s, ng * 512:(ng + 1) * 512],
                                                                            in1=tE[:, s, :], op=ALU.add), reads=[("tE", s), ("xt", s)], writes=[("xt", s)])

                proj_tok(OFF_OUT, 4, hT3, 0, "hT")
                dump(P, "x1", xt[:], [("xt", 0), ("xt", 1)])
                norm_to_hT(b, 1, ())
                dump(P, "h2T", hT[:], ["hT"])
                for j in range(NFF):
                    slg = wload(OFF_UP + 2 * j)
                    slu = wload(OFF_UP + 2 * j + 1)
                    bg = 4 * (j % 2)
                    bu = bg + 1
                    for (sl, bk) in ((slg, bg), (slu, bu)):
                        for c in range(16):
                            E("pe", lambda e, sl=sl, c=c, bk=bk: e.matmul(ps[bk][:, 0:T], lhsT=Wslot[sl][:, c * 128:(c + 1) * 128], rhs=hT3[:, c, :],
                                                                          start=(c == 0), stop=(c == 15)),
                              reads=[("W", sl), "hT"] if c in (0, 15) else (), writes=[("ps", bk)] if c in (0, 15) else (), mark=(c == 15))
                    t1 = tA[:, 10 + (j % 2), :]
                    sigmoid_from(t1, ps[bg][:, 0:T], t1, [("ps", bg)], "sg%d" % (j % 2))
                    E("dve", lambda e, t1=t1, bg=bg: e.tensor_tensor(out=t1, in0=t1, in1=ps[bg][:, 0:T], op=ALU.mult),
                      reads=["sg%d" % (j % 2), ("ps", bg)], writes=["sg%d" % (j % 2)])
                    first = ["R1ph"] if j == 0 else []
                    E("dve", lambda e, t1=t1, bu=bu, j=j: e.tensor_tensor(out=actT[:, j, :], in0=t1, in1=ps[bu][:, 0:T], op=ALU.mult),
                      reads=["sg%d" % (j % 2), ("ps", bu)] + ([] if j == 0 else ["R1ph"]), writes=["actT"] + first)
                dump(P, "actT", R1[:, 0:5632].bitcast(BF16), ["actT"])
                proj_tok(OFF_DN, 11, actT, 1, "actT")
                P.dma("pool", "xout", [(out_d[b, t0 + s * 128:t0 + (s + 1) * 128, :], xt3[:, s, :]) for s in range(2)],
                      reads=[("xt", 0), ("xt", 1)], writes=[("out", b, i)])
        fin = [v for k, v in P.last_w.items() if isinstance(k, tuple) and k[0] in ("out", "dbgout")]
        P.wait_all("pool", fin)
        P.run()
    P.close()
    return nc


_NC_CACHE = {}


def _stat_blocks(W):
    K, N = W.shape
    nch = N // 128
    return np.ascontiguousarray(W.reshape(16, 128, nch, 128).transpose(2, 1, 0, 3)).reshape(nch, 128, 2048)


def _mov_blocks(W):
    K, N = W.shape
    nkg = K // 512
    NG = N // 512
    return np.ascontiguousarray(W.reshape(nkg, 4, 128, NG, 512).transpose(3, 0, 2, 1, 4)).reshape(NG * nkg, 128, 2048)


def _fm(v):
    v = np.asarray(v, np.float32).reshape(-1, 128)
    return np.ascontiguousarray(v.T)


def kernel(x, c, w_ada, b_ada, g_mix, w_in, conv_w, conv_b, w_gate_a, b_gate_a, w_gate_x, b_gate_x,
           lru_logit, b_forget, g_q, g_k, g_out_rec, g_out_att, w_out, g_ffn, w_up, w_down):
    f = lambda a: np.asarray(a, np.float32)
    x = f(x); c = f(c)
    w_ada = f(w_ada)[0]; w_in = f(w_in)[0]; w_out = f(w_out)[0]; w_up = f(w_up)[0]; w_down = f(w_down)[0]
    gate_b = _stat_blocks(w_up[:, :DFF])
    up_b = _stat_blocks(w_up[:, DFF:])
    wsrc = np.concatenate([
        _stat_blocks(w_ada),
        _stat_blocks(w_in[:, 0:4096]),
        _mov_blocks(w_in[:, 4096:5120]),
        _mov_blocks(w_out),
        np.stack([gate_b, up_b], 1).reshape(2 * NFF, 128, 2048),
        _mov_blocks(w_down),
    ], 0)
    assert wsrc.shape[0] == N_ADA + N_CONV
    par = np.zeros((128, 160), np.float32)
    par[:, 0:16] = _fm(f(g_mix)[0])
    par[:, 16:32] = _fm(f(g_ffn)[0])
    par[:, 32:64] = f(conv_w)[0].reshape(4, 8, 128).transpose(2, 1, 0).reshape(128, 32)
    par[:, 64:72] = _fm(f(conv_b)[0])
    par[:, 72:80] = _fm(f(b_gate_a)[0])
    par[:, 80:88] = _fm(f(b_gate_x)[0])
    par[:, 88:96] = _fm(f(lru_logit)[0])
    par[:, 96:104] = _fm(f(g_out_rec)[0])
    par[:, 104:112] = _fm(f(g_out_att)[0])
    par[:, 112] = f(g_q)[0]
    par[:, 113] = f(g_k)[0]
    par[0:8, 114] = f(b_forget)[0]
    bada = _fm(f(b_ada)[0])
    wg = np.stack([f(w_gate_a)[0], f(w_gate_x)[0]], 0).transpose(2, 0, 1, 3).reshape(128, 2048)
    wf = w_in[:, 5120:5128].reshape(16, 128, 8).transpose(1, 0, 2).reshape(128, 128)
    con = np.concatenate([np.eye(128, dtype=np.float32), np.triu(np.ones((128, 128), np.float32)),
                          np.ones((128, 128), np.float32)], 1)
    n_cores = 8
    in_maps = []
    for ci in range(n_cores):
        c2 = c[2 * ci:2 * ci + 2]
        cT = c2.reshape(2, 16, 128).transpose(2, 1, 0).reshape(128, 32)
        small = np.ascontiguousarray(np.concatenate([par, bada, cT, wg, wf, con], 1).astype(np.float32))
        assert small.shape == (128, C_END)
        in_maps.append({"x": np.ascontiguousarray(x[2 * ci:2 * ci + 2]), "wsrc": wsrc, "small": small})
    if "nc" not in _NC_CACHE:
        _NC_CACHE["nc"] = build_nc()
    res = run_bass_kernel_spmd(_NC_CACHE["nc"], in_maps, core_ids=list(range(n_cores)))
    return np.concatenate([np.asarray(r["out"], np.float32) for r in res.results], 0)
```
